# Optimizing a Trainium2 kernel written in Bass

```python
import math
import jax
import jax.numpy as jnp
from jax import lax
import numpy as np

D_MODEL = 4096
BATCH = 32
SEQ = 256
DEPTH = 2
DEC_BATCH = 8
DEC_SEQ = 2048
PAST_LEN = 256

GRID_W = 64
EPS = 1e-6
HEAD_DIM = 128
ATT_HEADS = 16
ATT_KV_HEADS = 4
ATT_GROUPS = ATT_HEADS // ATT_KV_HEADS
ATT_WIDTH = ATT_HEADS * HEAD_DIM
WINDOW = 128
Q_BLOCK = 128
ROPE_THETA = 10000.0
ROPE_FREQS = HEAD_DIM // 4
GLA_HEADS = 8
GLA_DK = 128
GLA_DV = 128
GLA_WIDTH = GLA_HEADS * GLA_DV
GLA_LOWRANK = 16
GLA_TAU = 16.0
GLA_CHUNK = 64
LRU_WIDTH = 1024
LRU_BLOCKS = 8
LRU_BLOCK = LRU_WIDTH // LRU_BLOCKS
LRU_CONV = 4
CONV_LEFT = 2
CONV_RIGHT = LRU_CONV - 1 - CONV_LEFT
LRU_C = 8.0

MIX_WIDTH = ATT_WIDTH + GLA_WIDTH + LRU_WIDTH
D_FF = -(-8 * D_MODEL // (3 * 256)) * 256
IN_SIZES = (ATT_WIDTH, ATT_KV_HEADS * HEAD_DIM, ATT_KV_HEADS * HEAD_DIM,
            GLA_HEADS * GLA_DK, GLA_HEADS * GLA_DK, GLA_WIDTH, GLA_WIDTH, 2 * GLA_LOWRANK,
            LRU_WIDTH, LRU_WIDTH)
IN_WIDTH = sum(IN_SIZES)
IN_SPLITS = tuple(sum(IN_SIZES[:i + 1]) for i in range(len(IN_SIZES) - 1))

kernel_name = "hybrid_dit_prefix_context_step"

F32 = jnp.float32


def _rmsnorm(x, g):
    xf = x.astype(F32)
    y = xf * lax.rsqrt(jnp.mean(xf * xf, axis=-1, keepdims=True) + EPS)
    return (y * g.astype(F32)).astype(x.dtype)


def _axial_rope_tables(rows):
    row = jnp.repeat(jnp.arange(rows, dtype=F32), GRID_W)
    col = jnp.tile(jnp.arange(GRID_W, dtype=F32), rows)
    inv = ROPE_THETA ** (-jnp.arange(ROPE_FREQS, dtype=F32) / ROPE_FREQS)
    ang_r = row[:, None] * inv[None, :]
    ang_c = col[:, None] * inv[None, :]
    return (jnp.cos(ang_r), jnp.sin(ang_r), jnp.cos(ang_c), jnp.sin(ang_c))


def _rotate(v, cos, sin):
    c = cos[None, :, None, :]
    s = sin[None, :, None, :]
    v1, v2 = v[..., :ROPE_FREQS], v[..., ROPE_FREQS:]
    return jnp.concatenate([v1 * c - v2 * s, v1 * s + v2 * c], axis=-1)


def _apply_axial_rope(x, tables):
    cr, sr, cc, sc = tables
    xf = x.astype(F32)
    half = HEAD_DIM // 2
    y = jnp.concatenate([_rotate(xf[..., :half], cr, sr), _rotate(xf[..., half:], cc, sc)], axis=-1)
    return y.astype(x.dtype)


def _attend(q, k, v, mask, sink):
    s = jnp.einsum('bqkgd,bskd->bkgqs', q, k, preferred_element_type=F32) * (HEAD_DIM ** -0.5)
    if mask is not None:
        s = jnp.where(mask, s, -jnp.inf)
    sk = sink.astype(F32)[None, :, :, None, None]
    m = jnp.maximum(jnp.max(s, axis=-1, keepdims=True), sk)
    p = jnp.exp(s - m)
    denom = jnp.sum(p, axis=-1, keepdims=True) + jnp.exp(sk - m)
    return jnp.einsum('bkgqs,bskd->bqkgd', (p / denom).astype(v.dtype), v)


def _context_attention(q, k, v, sink):
    b, s = q.shape[:2]
    nb = s // Q_BLOCK
    qb = q.reshape(b, nb, Q_BLOCK, ATT_KV_HEADS, ATT_GROUPS, HEAD_DIM).transpose(1, 0, 2, 3, 4, 5)
    o = lax.map(lambda qi: _attend(qi, k, v, None, sink), qb)
    return o.transpose(1, 0, 2, 3, 4, 5).reshape(b, s, ATT_WIDTH)


def _latent_attention(q, k, v, k_ctx, v_ctx, sink):
    b, n = q.shape[:2]
    nb = n // Q_BLOCK
    span = Q_BLOCK + 2 * WINDOW
    ctx_len = k_ctx.shape[1]
    qb = q.reshape(b, nb, Q_BLOCK, ATT_KV_HEADS, ATT_GROUPS, HEAD_DIM).transpose(1, 0, 2, 3, 4, 5)
    pad = ((0, 0), (WINDOW, WINDOW), (0, 0), (0, 0))
    kp = jnp.pad(k, pad)
    vp = jnp.pad(v, pad)
    ctx_mask = jnp.ones((Q_BLOCK, ctx_len), dtype=bool)

    def one_block(args):
        i, qi = args
        start = i * Q_BLOCK
        kw = lax.dynamic_slice_in_dim(kp, start, span, axis=1)
        vw = lax.dynamic_slice_in_dim(vp, start, span, axis=1)
        qpos = start + jnp.arange(Q_BLOCK)
        kpos = start - WINDOW + jnp.arange(span)
        win = ((jnp.abs(qpos[:, None] - kpos[None, :]) <= WINDOW)
               & (kpos >= 0)[None, :] & (kpos < n)[None, :])
        mask = jnp.concatenate([win, ctx_mask], axis=1)
        return _attend(qi, jnp.concatenate([kw, k_ctx], axis=1),
                       jnp.concatenate([vw, v_ctx], axis=1), mask, sink)

    o = lax.map(one_block, (jnp.arange(nb), qb))
    return o.transpose(1, 0, 2, 3, 4, 5).reshape(b, n, ATT_WIDTH)


def _gla_scan(q, k, v, log_a, s0):
    b, n = q.shape[:2]
    nc = n // GLA_CHUNK

    def chunks(t):
        return t.reshape(b, nc, GLA_CHUNK, GLA_HEADS, t.shape[-1]).transpose(1, 0, 3, 2, 4)

    causal = jnp.tril(jnp.ones((GLA_CHUNK, GLA_CHUNK), dtype=bool))

    def step(state, inp):
        qc, kc, vc, ac = inp
        cum = jnp.cumsum(ac, axis=2)
        total = cum[:, :, -1:, :]
        q_in = qc * jnp.exp(cum)
        k_in = kc * jnp.exp(-cum)
        att = jnp.where(causal, jnp.einsum('bhtd,bhsd->bhts', q_in, k_in), 0.0)
        o = (jnp.einsum('bhts,bhsv->bhtv', att, vc)
             + jnp.einsum('bhtd,bhdv->bhtv', q_in, state))
        k_dec = kc * jnp.exp(total - cum)
        new_state = (jnp.exp(total)[:, :, 0, :, None] * state
                     + jnp.einsum('bhsd,bhsv->bhdv', k_dec, vc))
        return new_state, o

    final, o = lax.scan(step, s0, (chunks(q), chunks(k), chunks(v), chunks(log_a)))
    o = o.transpose(1, 0, 3, 2, 4).reshape(b, n, GLA_HEADS, GLA_DV)
    return o, final


def _gla_mixer(q, k, v, g, dec_lr, w_dec, b_dec, norm_g, init):
    b, n, _ = q.shape
    q = (q.astype(F32) * (GLA_DK ** -0.5)).reshape(b, n, GLA_HEADS, GLA_DK)
    k = k.astype(F32).reshape(b, n, GLA_HEADS, GLA_DK)
    v = v.astype(F32).reshape(b, n, GLA_HEADS, GLA_DV)
    lr = dec_lr.astype(F32).reshape(b, n, 2, GLA_LOWRANK)
    z = jnp.einsum('bndr,drc->bndc', lr, w_dec.astype(F32)) + b_dec.astype(F32)
    log_a = (jax.nn.log_sigmoid(z) / GLA_TAU).reshape(b, n, 2, GLA_HEADS, GLA_DK)
    o_f, s_f = _gla_scan(q, k, v, log_a[:, :, 0], init[0])
    flip = lambda t: jnp.flip(t, axis=1)
    o_b, s_b = _gla_scan(flip(q), flip(k), flip(v), flip(log_a[:, :, 1]), init[1])
    o = o_f + flip(o_b)
    o = (o * lax.rsqrt(jnp.mean(o * o, axis=-1, keepdims=True) + EPS)
         * norm_g.astype(F32).reshape(GLA_HEADS, GLA_DV))
    o = o.reshape(b, n, GLA_WIDTH) * jax.nn.silu(g.astype(F32))
    return o, jnp.stack([s_f, s_b], axis=1)


def _depthwise_conv(x, w, bias):
    y = lax.conv_general_dilated(x, w[:, None, :], window_strides=(1,),
                                 padding=[(CONV_LEFT, CONV_RIGHT)],
                                 dimension_numbers=('NWC', 'WIO', 'NWC'),
                                 feature_group_count=x.shape[-1])
    return y + bias


def _linear_combine(left, right):
    a_l, u_l = left
    a_r, u_r = right
    return a_l * a_r, a_r * u_l + u_r


def _rglru_scan(x, w_r, b_r, w_i, b_i, lam, h0):
    b, n, _ = x.shape
    xb = x.reshape(b, n, LRU_BLOCKS, LRU_BLOCK)
    r = jax.nn.sigmoid(jnp.einsum('bnki,kij->bnkj', xb, w_r.astype(F32)).reshape(b, n, LRU_WIDTH)
                       + b_r.astype(F32))
    i = jax.nn.sigmoid(jnp.einsum('bnki,kij->bnkj', xb, w_i.astype(F32)).reshape(b, n, LRU_WIDTH)
                       + b_i.astype(F32))
    log_a = -LRU_C * r * jax.nn.softplus(-lam.astype(F32))
    a = jnp.exp(log_a)
    u = jnp.sqrt(-jnp.expm1(2.0 * log_a)) * (i * x)
    u = u.at[:, 0].add(a[:, 0] * h0)
    _, h = lax.associative_scan(_linear_combine, (a, u), axis=1)
    return h, h[:, -1]


def _rglru_mixer(x, y, conv_w, conv_b, w_r, b_r, w_i, b_i, lam, init):
    xc = _depthwise_conv(x.astype(F32), conv_w.astype(F32), conv_b.astype(F32))
    h_f, s_f = _rglru_scan(xc, w_r[0], b_r[0], w_i[0], b_i[0], lam[0], init[0])
    h_b, s_b = _rglru_scan(jnp.flip(xc, axis=1), w_r[1], b_r[1], w_i[1], b_i[1], lam[1], init[1])
    h = h_f + jnp.flip(h_b, axis=1)
    out = h * jax.nn.gelu(y.astype(F32))
    return out, jnp.stack([s_f, s_b], axis=1)


def _token_mixers(h, mix_p, cached, rope):
    (w_in, sink, gla_w_dec, gla_b_dec, gla_norm, conv_w, conv_b,
     w_r, b_r, w_i, b_i, lam) = mix_p
    b, n, _ = h.shape
    proj = h @ w_in
    aq, ak, av, gq, gk, gv, gg, gdec, lx, ly = jnp.split(proj, IN_SPLITS, axis=-1)
    aq = aq.reshape(b, n, ATT_HEADS, HEAD_DIM)
    ak = ak.reshape(b, n, ATT_KV_HEADS, HEAD_DIM)
    av = av.reshape(b, n, ATT_KV_HEADS, HEAD_DIM)
    sink_g = sink.reshape(ATT_KV_HEADS, ATT_GROUPS)
    if cached is None:
        attn = _context_attention(aq, ak, av, sink_g)
        gla_init = jnp.zeros((2, b, GLA_HEADS, GLA_DK, GLA_DV), F32)
        lru_init = jnp.zeros((2, b, LRU_WIDTH), F32)
    else:
        k_ctx, v_ctx, gla_state, lru_state = cached
        attn = _latent_attention(_apply_axial_rope(aq, rope), _apply_axial_rope(ak, rope), av,
                                 k_ctx, v_ctx, sink_g)
        gla_init = jnp.moveaxis(gla_state.astype(F32), 1, 0)
        lru_init = jnp.moveaxis(lru_state.astype(F32), 1, 0)
    gla_out, gla_fin = _gla_mixer(gq, gk, gv, gg, gdec, gla_w_dec, gla_b_dec, gla_norm, gla_init)
    lru_out, lru_fin = _rglru_mixer(lx, ly, conv_w, conv_b, w_r, b_r, w_i, b_i, lam, lru_init)
    mix = jnp.concatenate([attn, gla_out.astype(h.dtype), lru_out.astype(h.dtype)], axis=-1)
    return mix, (ak, av, gla_fin, lru_fin)


def _layer(x, mod, n1, n2, mix_p, w_out, w_gu, w_down, cached, rope):
    sh1, sc1, g1, sh2, sc2, g2 = jnp.split(mod, 6, axis=-1)
    h = _rmsnorm(x, n1) * (1.0 + sc1) + sh1
    mix, ctx = _token_mixers(h, mix_p, cached, rope)
    x = x + g1 * (mix @ w_out)
    h = _rmsnorm(x, n2) * (1.0 + sc2) + sh2
    a, u = jnp.split(h @ w_gu, 2, axis=-1)
    x = x + g2 * ((jax.nn.silu(a) * u) @ w_down)
    return x, ctx


def setup_inputs(seed: int = 0) -> dict:
    key = jax.random.key(seed)
    ks = jax.random.split(key, 32)

    def nrm(k, shape, s):
        return jax.random.normal(k, shape, F32) * s

    u = jax.random.uniform(ks[22], (DEPTH, 2, LRU_WIDTH), F32, 0.9, 0.999)
    a = u ** (1.0 / LRU_C)
    lru_lambda = jnp.log(a) - jnp.log1p(-a)
    return {
        "x_prompt": nrm(ks[0], (BATCH, SEQ, D_MODEL), 1.0),
        "x_sample": nrm(ks[1], (DEC_BATCH, DEC_SEQ, D_MODEL), 1.0),
        "cache_attn_k": nrm(ks[2], (DEC_BATCH, DEPTH, PAST_LEN, ATT_KV_HEADS, HEAD_DIM), 1.0),
        "cache_attn_v": nrm(ks[3], (DEC_BATCH, DEPTH, PAST_LEN, ATT_KV_HEADS, HEAD_DIM), 1.0),
        "state_gla": nrm(ks[4], (DEC_BATCH, DEPTH, 2, GLA_HEADS, GLA_DK, GLA_DV), 0.5),
        "state_lru": nrm(ks[5], (DEC_BATCH, DEPTH, 2, LRU_WIDTH), 0.5),
        "c": nrm(ks[6], (DEC_BATCH, D_MODEL), 1.0),
        "c_ctx": nrm(ks[7], (D_MODEL,), 1.0),
        "w_ada": nrm(ks[8], (DEPTH, D_MODEL, 6 * D_MODEL), 0.5 * D_MODEL ** -0.5),
        "b_ada": nrm(ks[9], (DEPTH, 6 * D_MODEL), 0.02),
        "norm1_g": 1.0 + nrm(ks[10], (DEPTH, D_MODEL), 0.02),
        "w_in": nrm(ks[11], (DEPTH, D_MODEL, IN_WIDTH), D_MODEL ** -0.5),
        "attn_sink": nrm(ks[12], (DEPTH, ATT_HEADS), 0.5),
        "gla_w_decay": nrm(ks[13], (DEPTH, 2, GLA_LOWRANK, GLA_HEADS * GLA_DK), GLA_LOWRANK ** -0.5),
        "gla_b_decay": nrm(ks[14], (DEPTH, 2, GLA_HEADS * GLA_DK), 0.1),
        "gla_norm_g": 1.0 + nrm(ks[15], (DEPTH, GLA_WIDTH), 0.02),
        "lru_conv_w": nrm(ks[16], (DEPTH, LRU_CONV, LRU_WIDTH), LRU_CONV ** -0.5),
        "lru_conv_b": nrm(ks[17], (DEPTH, LRU_WIDTH), 0.02),
        "lru_w_rgate": nrm(ks[18], (DEPTH, 2, LRU_BLOCKS, LRU_BLOCK, LRU_BLOCK), LRU_BLOCK ** -0.5),
        "lru_b_rgate": nrm(ks[19], (DEPTH, 2, LRU_WIDTH), 0.02),
        "lru_w_igate": nrm(ks[20], (DEPTH, 2, LRU_BLOCKS, LRU_BLOCK, LRU_BLOCK), LRU_BLOCK ** -0.5),
        "lru_b_igate": nrm(ks[21], (DEPTH, 2, LRU_WIDTH), 0.02),
        "lru_lambda": lru_lambda,
        "w_out": nrm(ks[23], (DEPTH, MIX_WIDTH, D_MODEL), MIX_WIDTH ** -0.5),
        "norm2_g": 1.0 + nrm(ks[24], (DEPTH, D_MODEL), 0.02),
        "w_gu": nrm(ks[25], (DEPTH, D_MODEL, 2 * D_FF), D_MODEL ** -0.5),
        "w_down": nrm(ks[26], (DEPTH, D_FF, D_MODEL), D_FF ** -0.5),
        "final_norm_g": 1.0 + nrm(ks[27], (D_MODEL,), 0.02),
    }


def reference(x_prompt, x_sample, cache_attn_k, cache_attn_v, state_gla, state_lru, c, c_ctx,
              w_ada, b_ada, norm1_g, w_in, attn_sink, gla_w_decay, gla_b_decay, gla_norm_g,
              lru_conv_w, lru_conv_b, lru_w_rgate, lru_b_rgate, lru_w_igate, lru_b_igate,
              lru_lambda, w_out, norm2_g, w_gu, w_down, final_norm_g):
    rows = x_sample.shape[1] // GRID_W
    rope = _axial_rope_tables(rows)
    silu_c = jax.nn.silu(c)
    silu_ctx = jax.nn.silu(c_ctx)
    xc = x_prompt
    xl = x_sample
    new_k, new_v, new_gla, new_lru = [], [], [], []
    for l in range(DEPTH):
        mix_p = (w_in[l], attn_sink[l], gla_w_decay[l], gla_b_decay[l], gla_norm_g[l],
                 lru_conv_w[l], lru_conv_b[l], lru_w_rgate[l], lru_b_rgate[l],
                 lru_w_igate[l], lru_b_igate[l], lru_lambda[l])
        mod_ctx = (silu_ctx @ w_ada[l] + b_ada[l])[None, None, :]
        mod_lat = (silu_c @ w_ada[l] + b_ada[l])[:, None, :]
        xc, (k_l, v_l, g_l, r_l) = _layer(xc, mod_ctx, norm1_g[l], norm2_g[l], mix_p,
                                          w_out[l], w_gu[l], w_down[l], None, rope)
        new_k.append(k_l)
        new_v.append(v_l)
        new_gla.append(g_l)
        new_lru.append(r_l)
        cached = (cache_attn_k[:, l], cache_attn_v[:, l], state_gla[:, l], state_lru[:, l])
        xl, _ = _layer(xl, mod_lat, norm1_g[l], norm2_g[l], mix_p,
                       w_out[l], w_gu[l], w_down[l], cached, rope)
    y_prompt = _rmsnorm(xc, final_norm_g)
    y_sample = _rmsnorm(xl, final_norm_g)
    new_attn_k = jnp.stack(new_k, axis=1)
    new_attn_v = jnp.stack(new_v, axis=1)
    new_state_gla = jnp.stack(new_gla, axis=1)
    new_state_lru = jnp.stack(new_lru, axis=1)
    return (y_prompt, y_sample, new_attn_k, new_attn_v, new_state_gla, new_state_lru)
```

```python
import contextlib
import math
import numpy as np
import concourse.bass as bass
import concourse.mybir as mybir
from concourse.ap import AP
from concourse.bass_utils import run_bass_kernel_spmd

F32 = mybir.dt.float32
BF16 = mybir.dt.bfloat16
ALU = mybir.AluOpType
AF = mybir.ActivationFunctionType
EPS = 1e-6
NPHASES = 1000
SERIAL = 0
PESER = 0
NDMA_SP = 4
SERSET = ()
FV = 0


class Buf:
    __slots__ = ("name", "w", "r", "excl")

    def __init__(self, name="", excl=False):
        self.name = name
        self.w = None
        self.r = []
        self.excl = excl


class Em:
    NDMA = 3

    def __init__(self, nc, stack):
        self.nc = nc
        self.engs = {"pe": nc.tensor, "act": nc.scalar, "dve": nc.vector,
                     "pool": nc.gpsimd, "sp": nc.sync}
        self.sems = {}
        self.cnt = {}
        for k in self.engs:
            self.sems[k] = stack.enter_context(nc.semaphore("s_" + k))
            self.cnt[k] = 0
        self.dq = {}
        for q in ("sp", "pool", "act"):
            lst = []
            for i in range(self.NDMA if q != "sp" else NDMA_SP):
                key = "d_%s%d" % (q, i)
                self.sems[key] = stack.enter_context(nc.semaphore(key))
                self.cnt[key] = 0
                lst.append(key)
            self.dq[q] = [lst, 0]
        self.known = {k: {} for k in self.engs}
        self.n_ins = 0
        for key in self.sems:
            nc.gpsimd.sem_clear(self.sems[key])
        nc.all_engine_barrier()

    def finish(self):
        self.barrier()
        self.nc.all_engine_barrier()
        for key in self.sems:
            self.nc.gpsimd.sem_clear(self.sems[key])
        self.nc.all_engine_barrier()

    def _wait(self, e, evs):
        need = {}
        for ev in evs:
            if ev is None:
                continue
            k, v = ev
            if v > need.get(k, 0):
                need[k] = v
        kn = self.known[e]
        for k, v in need.items():
            if kn.get(k, 0) >= v:
                continue
            self.engs[e].wait_ge(self.sems[k], v)
            kn[k] = v
            self.n_ins += 1

    @staticmethod
    def _deps(reads, writes):
        evs = []
        for b in reads:
            evs.append(b.w)
            if b.excl:
                evs.extend(b.r)
        for b in writes:
            evs.append(b.w)
            evs.extend(b.r)
        return evs

    @staticmethod
    def _commit(ev, reads, writes):
        for b in reads:
            b.r.append(ev)
            if len(b.r) > 48:
                best = {}
                for k, v in b.r:
                    if v > best.get(k, 0):
                        best[k] = v
                b.r = list(best.items())
        for b in writes:
            b.w = ev
            b.r = []

    def op(self, e, fn, reads=(), writes=()):
        if PESER and e == "pe" and self.cnt["pe"] > 0:
            self._wait("pe", [("pe", self.cnt["pe"])])
        self._wait(e, self._deps(reads, writes))
        ins = fn()
        self.cnt[e] += 1
        ins.then_inc(self.sems[e], 1)
        ev = (e, self.cnt[e])
        self._commit(ev, reads, writes)
        self.n_ins += 1
        if SERIAL or e in SERSET:
            self.barrier()
        return ev

    def group(self, e, fns, reads=(), writes=()):
        if PESER and e == "pe" and self.cnt["pe"] > 0:
            self._wait("pe", [("pe", self.cnt["pe"])])
        self._wait(e, self._deps(reads, writes))
        ins = None
        for fn in fns:
            ins = fn()
            self.n_ins += 1
        self.cnt[e] += 1
        ins.then_inc(self.sems[e], 1)
        ev = (e, self.cnt[e])
        self._commit(ev, reads, writes)
        if SERIAL or e in SERSET:
            self.barrier()
        return ev

    def dma(self, q, out, in_, reads=(), writes=(), **kw):
        lst, idx = self.dq[q]
        key = lst[idx % len(lst)]
        self.dq[q][1] = idx + 1
        evs = self._deps(reads, writes)
        if self.cnt[key] > 0:
            evs.append((key, self.cnt[key]))
        self._wait(q, evs)
        ins = self.engs[q].dma_start(out=out, in_=in_, **kw)
        self.cnt[key] += 16
        ins.then_inc(self.sems[key], 16)
        ev = (key, self.cnt[key])
        self._commit(ev, reads, writes)
        self.n_ins += 1
        if SERIAL or ("dma_" + q) in SERSET:
            self.barrier()
        return ev

    def barrier(self):
        evs = [(k, v) for k, v in self.cnt.items() if v > 0]
        for e in self.engs:
            self._wait(e, evs)


class Cfg:
    def __init__(self, D=4096, AH=16, KV=4, GH=8, LB=8, DFF=11008, SEQ=256, NSEQ=4,
                 DSEQ=2048, PAST=256, L=2):
        self.D, self.AH, self.KV, self.GH, self.LB, self.DFF = D, AH, KV, GH, LB, DFF
        self.SEQ, self.NSEQ, self.DSEQ, self.PAST, self.L = SEQ, NSEQ, DSEQ, PAST, L
        self.KC = D // 128
        self.AW, self.KW, self.GW, self.LW = AH * 128, KV * 128, GH * 128, LB * 128
        self.MIX = self.AW + self.GW + self.LW
        self.KCM = self.MIX // 128
        self.FC = DFF // 128
        self.o_aq = 0
        self.o_ak = self.AW
        self.o_av = self.o_ak + self.KW
        self.o_gq = self.o_av + self.KW
        self.o_gk = self.o_gq + self.GW
        self.o_gv = self.o_gk + self.GW
        self.o_gg = self.o_gv + self.GW
        self.o_gd = self.o_gg + self.GW
        self.o_lx = self.o_gd + 32
        self.o_ly = self.o_lx + self.LW
        self.INW = self.o_ly + self.LW
        self.NCT = NSEQ * SEQ
        self.NTOK = self.NCT + DSEQ
        self.f_aq = 0
        self.f_ak = self.f_aq + AH
        self.f_gq = self.f_ak + KV
        self.f_gk = self.f_gq + GH
        self.f_gg = self.f_gk + GH
        self.f_gd = self.f_gg + GH
        self.f_lx = self.f_gd + 1
        self.f_ly = self.f_lx + LB
        self.NF = self.f_ly + LB
        self.PTW = self.KW + self.GW
        self.vcol = {}
        n = 0

        def add(name, c):
            nonlocal n
            self.vcol[name] = n
            n += c
        add("cT", 2 * self.KC)
        add("fng", self.KC)
        for l in range(L):
            add("bada%d" % l, 6 * self.KC)
            add("n1g%d" % l, self.KC)
            add("n2g%d" % l, self.KC)
            add("sink%d" % l, AH)
            add("bdec%d" % l, 2 * GH)
            add("gnorm%d" % l, GH)
            add("convw%d" % l, 4 * LB)
            add("convb%d" % l, LB)
            add("br%d" % l, 2 * LB)
            add("bi%d" % l, 2 * LB)
            add("lam%d" % l, 2 * LB)
            add("lru0%d" % l, 2 * LB)
        self.NV = n

    def tiles(self):
        out = []
        for t0 in range(0, self.NCT, 512):
            out.append((t0, 0))
        for t0 in range(self.NCT, self.NTOK, 512):
            out.append((t0, 1))
        return out

    def tile_pairs(self, n=2):
        out = []
        for r in (0, 1):
            ts = [t0 for (t0, rr) in self.tiles() if rr == r]
            for i in range(0, len(ts), n):
                out.append((ts[i:i + n], r))
        return out

    def seqs(self):
        out = [(s * self.SEQ, self.SEQ, False, s) for s in range(self.NSEQ)]
        out.append((self.NCT, self.DSEQ, True, 0))
        return out


def feat_major(v):
    v = np.asarray(v, np.float32).reshape(-1, 128)
    return np.ascontiguousarray(v.T)


def mk(a, off, dims):
    return AP(a.tensor, a.offset + off, [list(a.ap[0])] + [list(d) for d in dims])


def bc_mid(a, reps):
    return AP(a.tensor, a.offset, [list(a.ap[0]), [0, reps], list(a.ap[1])])


def bc_last(a, n):
    return AP(a.tensor, a.offset, [list(a.ap[0]), list(a.ap[1]), [0, n]])


def rev(a):
    st, n = a.ap[1]
    return AP(a.tensor, a.offset + st * (n - 1), [list(a.ap[0]), [-st, n]])


class Slots:
    def __init__(self, K, name, shape, dt, n):
        self.t = [K.sb("%s%d" % (name, i), shape, dt) for i in range(n)]
        self.b = [Buf("%s%d" % (name, i)) for i in range(n)]
        self.i = 0

    def next(self):
        j = self.i % len(self.t)
        self.i += 1
        return self.t[j], self.b[j]


class K:
    pass


def build(cfg):
    c = cfg
    nc = bass.Bass("TRN2", target_bir_lowering=False)
    k = K()
    k.nc, k.c = nc, c
    L = c.L

    def din(name, shape, dt=F32):
        return nc.dram_tensor(name, list(shape), dt, kind="ExternalInput").ap()

    def dout(name, shape, dt=F32):
        return nc.dram_tensor(name, list(shape), dt, kind="ExternalOutput").ap()

    def dscr(name, shape, dt):
        return nc.dram_tensor(name, list(shape), dt).ap()

    k.xin = din("xin", [c.NTOK, c.D])
    k.ck = din("ck", [L, c.PAST, c.KW])
    k.cv = din("cv", [L, c.PAST, c.KW])
    k.sg = din("sg", [L, 2, c.GH, 128, 128])
    k.vecs = din("vecs", [128, c.NV])
    k.w_ada = din("w_ada", [L, c.D, 6 * c.D])
    k.w_in = din("w_in", [L, c.D, c.INW])
    k.w_dec = din("w_dec", [L, 2, 16, c.GW])
    k.w_r = din("w_r", [L, 2, c.LB, 128, 128])
    k.w_i = din("w_i", [L, 2, c.LB, 128, 128])
    k.w_out = din("w_out", [L, c.MIX, c.D])
    k.w_gu = din("w_gu", [L, c.D, 2 * c.DFF])
    k.w_down = din("w_down", [L, c.DFF, c.D])
    k.c_ident = din("c_ident", [128, 128])
    k.c_pm = din("c_pm", [128, 128])
    k.c_rope = din("c_rope", [128, 2, c.DSEQ])
    k.c_mlr = din("c_mlr", [128, 2, 512])
    k.c_gmask = din("c_gmask", [64, 2, 512])
    k.c_reset = din("c_reset", [128, 2048])

    k.y = dout("y", [c.NTOK, c.D])
    k.nk = dout("nk", [c.NSEQ, L, c.SEQ, c.KW])
    k.nv = dout("nv", [c.NSEQ, L, c.SEQ, c.KW])
    k.ngla = dout("ngla", [c.NSEQ, L, 2, c.GH, 128, 128])
    k.nlru = dout("nlru", [c.NSEQ, L, 2, c.LW])

    k.XT = dscr("XT", [c.KC, 128, c.NTOK], F32)
    k.HN = dscr("HN", [c.KC, 128, c.NTOK], BF16)
    k.PF = dscr("PF", [c.NF, 128, c.NTOK], BF16)
    k.PT = dscr("PT", [c.NTOK, c.PTW], BF16)
    k.MX = dscr("MX", [c.KCM, 128, c.NTOK], BF16)
    k.AC = dscr("AC", [c.FC, 128, c.NTOK], BF16)
    nt = c.NTOK // 512
    k.bXT = [Buf("XT%d" % i) for i in range(nt)]
    k.bHN = [Buf("HN%d" % i) for i in range(nt)]
    k.bPF = [Buf("PF%d" % i) for i in range(nt)]
    k.bPT = [Buf("PT%d" % i) for i in range(nt)]
    k.bMX = [Buf("MX%d" % i) for i in range(nt)]
    k.bAC = [Buf("AC%d" % i) for i in range(nt)]

    with contextlib.ExitStack() as gst:
        em = Em(nc, gst)
        k.em = em
        k.gst = gst
        k.st = gst
        k.uid = 0

        def _sb(name, shape, dt):
            k.uid += 1
            return k.st.enter_context(nc.sbuf_tensor("%s_%d" % (name, k.uid), list(shape), dt))
        k.sb = _sb
        k.pb = [gst.enter_context(nc.psum_tensor("pb%d" % i, [128, 512], F32)) for i in range(8)]
        k.bpb = [Buf("pb%d" % i, excl=True) for i in range(8)]
        k.pi = 0
        k.vec = k.sb("vec", [128, c.NV], F32)
        k.bvec = Buf("vec")
        em.dma("sp", k.vec[:], k.vecs, writes=[k.bvec])
        k.identf = k.sb("identf", [128, 128], F32)
        k.identb = k.sb("identb", [128, 128], BF16)
        k.onesb = k.sb("onesb", [128, 128], BF16)
        k.bconst = Buf("const")
        em.dma("sp", k.identf[:], k.c_ident, writes=[k.bconst])
        em.op("dve", lambda: nc.vector.tensor_copy(k.identb[:], k.identf[:]), reads=[k.bconst], writes=[k.bconst])
        em.op("dve", lambda: nc.vector.memset(k.onesb[:], 1.0), writes=[k.bconst])
        k.epst = k.sb("epst", [128, 1], F32)
        k.onet = k.sb("onet", [128, 1], F32)
        em.op("dve", lambda: nc.vector.memset(k.epst[:], EPS), writes=[k.bconst])
        em.op("dve", lambda: nc.vector.memset(k.onet[:], 1.0), writes=[k.bconst])
        k.mod = k.sb("mod", [128, L, 6 * c.KC, 2], F32)
        k.A = k.sb("modA", [128, L, 2, c.KC, 2], F32)
        k.bmod = Buf("mod")
        k.nbanks = 8
        k.sc = k.sb("ada_sc", [128, c.KC, 2], BF16)
        k.bsc = Buf("ada_sc")

        plist = [lambda: phase_transpose_in(k), lambda: phase_ada(k)]
        for l in range(L):
            plist += [lambda l=l: phase_norm(k, l, 0), lambda l=l: phase_inproj(k, l), lambda l=l: phase_attn(k, l),
                      lambda l=l: phase_gla(k, l), lambda l=l: phase_lru(k, l), lambda l=l: phase_outproj(k, l),
                      lambda l=l: phase_norm(k, l, 1), lambda l=l: phase_gu(k, l), lambda l=l: phase_down(k, l)]
        plist.append(lambda: phase_final(k))
        for f in plist[:NPHASES]:
            f()
        em.finish()
        print("EMSTAT n_ins", em.n_ins, {kk: vv for kk, vv in em.cnt.items()}, flush=True)
    return nc


def vcol(k, name, j=0, n=1):
    o = k.c.vcol[name] + j
    return k.vec[:, o:o + n]


def pbank(k):
    i = k.pi % k.nbanks
    k.pi += 1
    return k.pb[i], k.bpb[i]


@contextlib.contextmanager
def phase(k):
    with contextlib.ExitStack() as st:
        old = k.st
        k.st = st
        k.em.barrier()
        yield
        k.em.barrier()
        k.st = old


def phase_transpose_in(k):
    nc, em, c = k.nc, k.em, k.c
    with phase(k):
        xs = Slots(k, "ti_x", [128, c.D], F32, 2)
        xo = Slots(k, "ti_o", [128, c.KC, 512], F32, 2)
        ei = 0
        for (t0, r) in c.tiles():
            ot, bo = xo.next()
            for blk in range(4):
                tok = t0 + blk * 128
                xt, bx = xs.next()
                em.dma("sp", xt[:], k.xin[tok:tok + 128, :], writes=[bx])
                for c0 in range(0, c.KC, 4):
                    n = min(4, c.KC - c0)
                    pb, bp = pbank(k)
                    em.group("pe", [
                        (lambda j=j, pb=pb: nc.tensor.transpose(pb[:, j * 128:(j + 1) * 128],
                                                                xt[:, (c0 + j) * 128:(c0 + j + 1) * 128], k.identf[:]))
                        for j in range(n)], reads=[bx, k.bconst], writes=[bp])
                    evac(k, ei, ot[:, c0:c0 + n, blk * 128:(blk + 1) * 128],
                         pb[:, 0:n * 128].rearrange("p (c t) -> p c t", c=n), [bp], [bo])
                    ei += 1
            em.dma("pool", k.XT[:, :, t0:t0 + 512].rearrange("c p t -> p c t"), ot[:], reads=[bo], writes=[k.bXT[t0 // 512]])


def ada_epilogue(k, l, pb, bp):
    nc, em, c = k.nc, k.em, k.c
    KC = c.KC
    bada = vcol(k, "bada%d" % l, 0, 6 * KC)
    for r in range(2):
        em.op("dve", lambda r=r: nc.vector.tensor_tensor(
            k.mod[:, l, :, r], pb[:, 0:12 * KC].rearrange("p (m r) -> p m r", r=2)[:, :, r], bada, ALU.add),
            reads=[bp, k.bvec], writes=[k.bmod])
    for which in range(2):
        g = vcol(k, ("n1g%d" if which == 0 else "n2g%d") % l, 0, KC)
        base = KC + which * 3 * KC
        for r in range(2):
            em.op("dve", lambda which=which, r=r, g=g, base=base: nc.vector.scalar_tensor_tensor(
                k.A[:, l, which, :, r], k.mod[:, l, base:base + KC, r], 1.0, g, ALU.add, ALU.mult),
                reads=[k.bmod, k.bvec], writes=[k.bmod])


def phase_ada(k):
    nc, em, c = k.nc, k.em, k.c
    KC = c.KC
    with phase(k):
        sc, bsc = k.sc, k.bsc
        cT = vcol(k, "cT", 0, 2 * KC).rearrange("p (c r) -> p c r", r=2)
        em.op("act", lambda: nc.scalar.activation(sc[:], cT, AF.Silu), reads=[k.bvec], writes=[bsc])
        ws = Slots(k, "ada_w", [128, KC, 512], BF16, 3)
        NG = (6 * c.D) // 512
        for l in range(1):
            pb, bp = pbank(k)
            wv = k.w_ada[l].rearrange("(c p) n -> p c n", p=128)
            for g in range(NG):
                wt, bw = ws.next()
                em.dma("pool", wt[:], wv[:, :, g * 512:(g + 1) * 512], writes=[bw])
                for j in range(4):
                    m = g * 4 + j
                    em.group("pe", [
                        (lambda kc=kc, j=j, m=m, wt=wt, pb=pb: nc.tensor.matmul(
                            pb[:, 2 * m:2 * m + 2], lhsT=wt[:, kc, j * 128:(j + 1) * 128], rhs=sc[:, kc, :],
                            start=(kc == 0), stop=(kc == KC - 1)))
                        for kc in range(KC)], reads=[bw, bsc], writes=[bp])
            ada_epilogue(k, l, pb, bp)


def ada_bg(k, l, ws, pb, bp):
    nc, em, c = k.nc, k.em, k.c
    KC = c.KC
    n = 6 * KC
    wv = k.w_ada[l].rearrange("(c p) n -> p c n", p=128)
    tiles = {}

    def load(m):
        wt, bw = ws.next()
        em.dma("pool", wt[:], wv[:, :, m * 128:(m + 1) * 128], writes=[bw])
        tiles[m] = (wt, bw)
    for m in range(min(2, n)):
        load(m)
    for m in range(n):
        if m + 2 < n:
            load(m + 2)
        wt, bw = tiles.pop(m)
        em.group("pe", [
            (lambda kc=kc: nc.tensor.matmul(pb[:, 2 * m:2 * m + 2], lhsT=wt[:, kc, :], rhs=k.sc[:, kc, :],
                                            start=(kc == 0), stop=(kc == KC - 1)))
            for kc in range(KC)], reads=[bw, k.bsc], writes=[bp])
        yield
    ada_epilogue(k, l, pb, bp)


def norm_block(k, xt, bx, n, sq, bsq, rs, brs):
    nc, em, c = k.nc, k.em, k.c
    em.op("act", lambda: nc.scalar.activation(sq[:, :, 0:n], xt[:, :, 0:n], AF.Square), reads=[bx], writes=[bsq])
    pb, bp = pbank(k)
    em.group("pe", [
        (lambda kc=kc: nc.tensor.matmul(pb[:, 0:n], lhsT=k.onesb[:], rhs=sq[:, kc, 0:n],
                                        start=(kc == 0), stop=(kc == c.KC - 1)))
        for kc in range(c.KC)], reads=[bsq, k.bconst], writes=[bp])
    em.op("act", lambda: nc.scalar.activation(rs[:, 0:n], pb[:, 0:n], AF.Ln, bias=k.epst[:], scale=1.0 / c.D),
          reads=[bp, k.bconst], writes=[brs])
    em.op("act", lambda: nc.scalar.activation(rs[:, 0:n], rs[:, 0:n], AF.Exp, scale=-0.5), reads=[brs], writes=[brs])


def phase_norm(k, l, which):
    nc, em, c = k.nc, k.em, k.c
    KC = c.KC
    with phase(k):
        xs = Slots(k, "nm_x", [128, KC, 512], F32, 2)
        hs = Slots(k, "nm_h", [128, KC, 512], BF16, 2)
        rss = Slots(k, "nm_r", [128, 512], F32, 2)
        for (t0, r) in c.tiles():
            ti = t0 // 512
            xt, bx = xs.next()
            em.dma("sp", xt[:], k.XT[:, :, t0:t0 + 512].rearrange("c p t -> p c t"), reads=[k.bXT[ti]], writes=[bx])
            ht, bh = hs.next()
            rs, brs = rss.next()
            norm_block(k, xt, bx, 512, ht, bh, rs, brs)
            em.op("dve", lambda: nc.vector.tensor_tensor(xt[:], xt[:], bc_mid(rs[:], KC), ALU.mult),
                  reads=[bx, brs], writes=[bx])
            for kc in range(KC):
                em.op("act", lambda kc=kc: nc.scalar.activation(
                    ht[:, kc, :], xt[:, kc, :], AF.Identity,
                    bias=k.mod[:, l, which * 3 * KC + kc, r:r + 1], scale=k.A[:, l, which, kc, r:r + 1]),
                    reads=[bx, k.bmod], writes=[bh])
            em.dma("pool", k.HN[:, :, t0:t0 + 512].rearrange("c p t -> p c t"), ht[:], reads=[bh], writes=[k.bHN[ti]])


def load_act_tile(k, slots, src, bsrc, KCin, t0):
    at, ba = slots.next()
    k.em.dma("sp", at[:, 0:KCin, :], src[:, :, t0:t0 + 512].rearrange("c p t -> p c t"),
             reads=[bsrc[t0 // 512]], writes=[ba])
    return at, ba


_XB = {}


def load_w(k, slots, W, KCin, c0, n):
    wt, bw = slots.next()
    src = W[:, c0:c0 + n].rearrange("(c p) n -> p c n", p=128)
    if KCin <= 32:
        k.em.dma("pool", wt[:, 0:KCin, 0:n], src, writes=[bw])
        return wt, bw
    bw2 = _XB.setdefault(id(bw), Buf("xb"))
    h = KCin // 2
    k.em.dma("pool", wt[:, 0:h, 0:n], src[:, 0:h, :], writes=[bw])
    k.em.dma("pool", wt[:, h:KCin, 0:n], src[:, h:KCin, :], writes=[bw2])
    return wt, (bw, bw2)


def mm_fm(k, wt, bw, at, ba, KCin, j, M=128):
    nc = k.nc
    pb, bp = pbank(k)
    k.em.group("pe", [
        (lambda kc=kc: nc.tensor.matmul(pb[0:M, :], lhsT=wt[:, kc, j * 128:j * 128 + M], rhs=at[:, kc, :],
                                        start=(kc == 0), stop=(kc == KCin - 1)))
        for kc in range(KCin)], reads=(list(bw) if isinstance(bw, tuple) else [bw]) + [ba], writes=[bp])
    return pb, bp


def mm_tm(k, wt, bw, at, ba, KCin, tb, n):
    nc = k.nc
    pb, bp = pbank(k)
    k.em.group("pe", [
        (lambda kc=kc: nc.tensor.matmul(pb[:, 0:n], lhsT=at[:, kc, tb * 128:(tb + 1) * 128], rhs=wt[:, kc, 0:n],
                                        start=(kc == 0), stop=(kc == KCin - 1)))
        for kc in range(KCin)], reads=[bw, ba], writes=[bp])
    return pb, bp


def evac(k, i, dst, src, reads, writes):
    nc = k.nc
    if i % 2 == 0:
        k.em.op("act", lambda: nc.scalar.copy(dst, src), reads=reads, writes=writes)
    else:
        k.em.op("dve", lambda: nc.vector.tensor_copy(dst, src), reads=reads, writes=writes)


def phase_inproj(k, l):
    nc, em, c = k.nc, k.em, k.c
    KC = c.KC
    W = k.w_in[l]
    fm = []

    def add_fm(o, width, f):
        for g0 in range(0, width, 512):
            n = min(512, width - g0)
            fm.append((o + g0, n, f + g0 // 128))
    add_fm(c.o_aq, c.AW, c.f_aq)
    add_fm(c.o_ak, c.KW, c.f_ak)
    add_fm(c.o_gq, c.GW, c.f_gq)
    add_fm(c.o_gk, c.GW, c.f_gk)
    add_fm(c.o_gg, c.GW, c.f_gg)
    add_fm(c.o_lx, c.LW, c.f_lx)
    add_fm(c.o_ly, c.LW, c.f_ly)
    tm = []
    for g0 in range(0, c.KW, 512):
        n = min(512, c.KW - g0)
        tm.append((c.o_av + g0, n, g0, "nv"))
    for g0 in range(0, c.GW, 512):
        n = min(512, c.GW - g0)
        tm.append((c.o_gv + g0, n, c.KW + g0, None))
    tmk = []
    for g0 in range(0, c.KW, 512):
        n = min(512, c.KW - g0)
        tmk.append((c.o_ak + g0, n, g0, "nk"))
    with phase(k):
        acts = Slots(k, "ip_a", [128, KC, 512], BF16, 2)
        ws = Slots(k, "ip_w", [128, KC, 512], BF16, 3)
        stg = Slots(k, "ip_s", [128, 4, 512], BF16, 3)
        stt = Slots(k, "ip_t", [128, 512], BF16, 3)
        stf = Slots(k, "ip_f", [128, 512], F32, 3)
        ei = 0
        for (t0s, r) in c.tile_pairs():
            ats = [load_act_tile(k, acts, k.HN, k.bHN, KC, t0) for t0 in t0s]
            for (o, n, f) in fm:
                wt, bw = load_w(k, ws, W, KC, o, n)
                for t0, (at, ba) in zip(t0s, ats):
                    ti = t0 // 512
                    sg_, bs = stg.next()
                    nj = n // 128
                    for j in range(nj):
                        pb, bp = mm_fm(k, wt, bw, at, ba, KC, j)
                        evac(k, ei, sg_[:, j, :], pb[:], [bp], [bs])
                        ei += 1
                    em.dma("sp", k.PF[f:f + nj, :, t0:t0 + 512].rearrange("c p t -> p c t"), sg_[:, 0:nj, :],
                           reads=[bs], writes=[k.bPF[ti]])
            wt, bw = load_w(k, ws, W, KC, c.o_gd, 32)
            for t0, (at, ba) in zip(t0s, ats):
                ti = t0 // 512
                sg_, bs = stg.next()
                pb, bp = mm_fm(k, wt, bw, at, ba, KC, 0, M=32)
                evac(k, ei, sg_[0:32, 0, :], pb[0:32, :], [bp], [bs])
                ei += 1
                em.dma("sp", k.PF[c.f_gd, 0:32, t0:t0 + 512], sg_[0:32, 0, :], reads=[bs], writes=[k.bPF[ti]])
            groups = list(tm) + (list(tmk) if r == 0 else [])
            for (o, n, pc, oname) in groups:
                wt, bw = load_w(k, ws, W, KC, o, n)
                for (t0, tb, at, ba) in [(t0, tb, at, ba) for t0, (at, ba) in zip(t0s, ats) for tb in range(4)]:
                    ti = t0 // 512
                    pb, bp = mm_tm(k, wt, bw, at, ba, KC, tb, n)
                    tok = t0 + tb * 128
                    if oname != "nk":
                        st_, bt_ = stt.next()
                        evac(k, ei, st_[:, 0:n], pb[:, 0:n], [bp], [bt_])
                        ei += 1
                        em.dma("sp", k.PT[tok:tok + 128, pc:pc + n], st_[:, 0:n], reads=[bt_], writes=[k.bPT[ti]])
                    if oname is not None and r == 0:
                        sf_, bf_ = stf.next()
                        evac(k, ei, sf_[:, 0:n], pb[:, 0:n], [bp], [bf_])
                        ei += 1
                        s_i, s_o = tok // c.SEQ, tok % c.SEQ
                        dst = (k.nv if oname == "nv" else k.nk)[s_i, l, s_o:s_o + 128, pc:pc + n]
                        em.dma("sp", dst, sf_[:, 0:n], reads=[bf_])


def phase_attn(k, l):
    nc, em, c = k.nc, k.em, k.c
    scale = 128.0 ** -0.5
    NQ = c.DSEQ
    with phase(k):
        esink = k.sb("at_es", [128, c.AH], F32)
        bes = Buf("es")
        em.op("act", lambda: nc.scalar.activation(esink[:], vcol(k, "sink%d" % l, 0, c.AH), AF.Exp),
              reads=[k.bvec], writes=[bes])
        mlr_f = k.sb("at_mf", [128, 2, 512], F32)
        mlr = k.sb("at_mb", [128, 2, 512], BF16)
        bm = Buf("mlr")
        em.dma("sp", mlr_f[:], k.c_mlr, writes=[bm])
        em.op("dve", lambda: nc.vector.tensor_copy(mlr[:], mlr_f[:]), reads=[bm], writes=[bm])
        pmf = k.sb("at_pf", [128, 128], F32)
        pmb = k.sb("at_pb", [128, 128], BF16)
        em.dma("sp", pmf[:], k.c_pm, writes=[bm])
        em.op("dve", lambda: nc.vector.tensor_copy(pmb[:], pmf[:]), reads=[bm], writes=[bm])
        rope = k.sb("at_rp", [128, 2, NQ], F32)
        em.dma("sp", rope[:], k.c_rope, writes=[bm])
        qs = Slots(k, "at_q", [128, 4, NQ], BF16, 2)
        ks = Slots(k, "at_k", [128, NQ], BF16, 2)
        vs = Slots(k, "at_v", [128, NQ // 128, 128], BF16, 2)
        qr = Slots(k, "at_qr", [128, 4, NQ], BF16, 1)
        kr = Slots(k, "at_kr", [128, NQ], BF16, 1)
        outs = Slots(k, "at_o", [128, 4, NQ], BF16, 2)
        pts = Slots(k, "at_p", [128, 5, 512], BF16, 2)
        dens = Slots(k, "at_d", [128, 512], F32, 2)
        t1s = Slots(k, "at_t1", [128, 512], F32, 2)
        t2s = Slots(k, "at_t2", [128, 512], F32, 2)
        kcf = Slots(k, "at_kcf", [128, 2, 128], F32, 1)
        kct = Slots(k, "at_kct", [128, c.PAST], BF16, 1)
        vcs = Slots(k, "at_vc", [128, c.PAST // 128, 128], BF16, 1)

        def rope_apply(dst, src, bsrc, bdst, n0, n):
            pb, bp = pbank(k)
            em.op("pe", lambda: nc.tensor.matmul(pb[:, 0:n], lhsT=pmb[:], rhs=src, start=True, stop=True),
                  reads=[bsrc, bm], writes=[bp])
            t1, b1 = t1s.next()
            t2, b2 = t2s.next()
            em.op("pool", lambda: nc.gpsimd.tensor_tensor(t1[:, 0:n], src, rope[:, 0, n0:n0 + n], ALU.mult),
                  reads=[bsrc, bm], writes=[b1])
            em.op("dve", lambda: nc.vector.tensor_tensor(t2[:, 0:n], pb[:, 0:n], rope[:, 1, n0:n0 + n], ALU.mult),
                  reads=[bp, bm], writes=[b2])
            em.op("dve", lambda: nc.vector.tensor_tensor(dst, t1[:, 0:n], t2[:, 0:n], ALU.add),
                  reads=[b1, b2], writes=[bdst])

        def attend(Q, bq, qb, chunks, OUT, bo, kv):
            n = len(chunks)
            pt, bpt = pts.next()
            rhs = Q[:, :, qb * 128:(qb + 1) * 128]
            for i, (kT, V, mi, bk, bv) in enumerate(chunks):
                pb, bp = pbank(k)
                out = pb[:].rearrange("p (g q) -> p g q", g=4)
                fns = [lambda out=out, kT=kT, mi=mi: nc.tensor.matmul(out, lhsT=kT, rhs=rhs, start=True, stop=(mi is None))]
                if mi is not None:
                    fns.append(lambda pb=pb, mi=mi: nc.tensor.matmul(pb[:], lhsT=k.identb[:], rhs=mlr[:, mi, :],
                                                                    start=False, stop=True))
                em.group("pe", fns, reads=[bq, bk, bm, k.bconst], writes=[bp])
                em.op("act", lambda i=i, pb=pb: nc.scalar.activation(pt[:, i, :], pb[:], AF.Exp, scale=scale),
                      reads=[bp], writes=[bpt])
            po, bpo = pbank(k)
            em.group("pe", [
                (lambda i=i, V=chunks[i][1]: nc.tensor.matmul(po[:], lhsT=V, rhs=pt[:, i, :],
                                                              start=(i == 0), stop=(i == n - 1)))
                for i in range(n)], reads=[bpt] + [ch[4] for ch in chunks], writes=[bpo])
            pd, bpd = pbank(k)
            em.group("pe", [
                (lambda i=i: nc.tensor.matmul(pd[:], lhsT=k.onesb[:], rhs=pt[:, i, :],
                                              start=(i == 0), stop=(i == n - 1)))
                for i in range(n)], reads=[bpt, k.bconst], writes=[bpd])
            dn, bd = dens.next()
            for g in range(4):
                h = kv * 4 + g
                em.op("dve", lambda g=g, h=h: nc.vector.tensor_scalar(
                    dn[:, g * 128:(g + 1) * 128], pd[:, g * 128:(g + 1) * 128], esink[:, h:h + 1], None, ALU.add),
                    reads=[bpd, bes], writes=[bd])
            em.op("dve", lambda: nc.vector.reciprocal(dn[:], dn[:]), reads=[bd], writes=[bd])
            em.op("dve", lambda: nc.vector.tensor_tensor(
                OUT[:, :, qb * 128:(qb + 1) * 128], po[:].rearrange("p (g q) -> p g q", g=4),
                dn[:].rearrange("p (g q) -> p g q", g=4), ALU.mult), reads=[bpo, bd], writes=[bo])

        for (t0, N, lat, si) in c.seqs():
            tis = list(range(t0 // 512, (t0 + N + 511) // 512))
            rPF = [k.bPF[i] for i in tis]
            rPT = [k.bPT[i] for i in tis]
            for kv in range(c.KV):
                Q, bq = qs.next()
                kT, bk = ks.next()
                V, bv = vs.next()
                em.dma("sp", Q[:, :, 0:N], k.PF[c.f_aq + kv * 4:c.f_aq + kv * 4 + 4, :, t0:t0 + N]
                       .rearrange("c p t -> p c t"), reads=rPF, writes=[bq])
                em.dma("sp", kT[:, 0:N], k.PF[c.f_ak + kv, :, t0:t0 + N], reads=rPF, writes=[bk])
                em.dma("sp", V[:, 0:N // 128, :], k.PT[t0:t0 + N, kv * 128:(kv + 1) * 128]
                       .rearrange("(b p) d -> p b d", p=128), reads=rPT, writes=[bv])
                OUT, bo = outs.next()
                if not lat:
                    for qb in range(N // 128):
                        chunks = [(kT[:, i * 128:(i + 1) * 128], V[:, i, :], None, bk, bv) for i in range(N // 128)]
                        attend(Q, bq, qb, chunks, OUT, bo, kv)
                else:
                    Qr, bqr = qr.next()
                    Kr, bkr = kr.next()
                    for n0 in range(0, N, 512):
                        for g in range(4):
                            rope_apply(Qr[:, g, n0:n0 + 512], Q[:, g, n0:n0 + 512], bq, bqr, n0, 512)
                        rope_apply(Kr[:, n0:n0 + 512], kT[:, n0:n0 + 512], bk, bkr, n0, 512)
                    kc_f, bkc = kcf.next()
                    kc_t, bkt = kct.next()
                    vc, bvc = vcs.next()
                    em.dma("sp", kc_f[:], k.ck[l, :, kv * 128:(kv + 1) * 128].rearrange("(b p) d -> p b d", p=128),
                           writes=[bkc])
                    em.dma("pool", vc[:], k.cv[l, :, kv * 128:(kv + 1) * 128].rearrange("(b p) d -> p b d", p=128),
                           writes=[bvc])
                    for b in range(c.PAST // 128):
                        pb, bp = pbank(k)
                        em.op("pe", lambda b=b, pb=pb: nc.tensor.transpose(pb[:, 0:128], kc_f[:, b, :], k.identf[:]),
                              reads=[bkc, k.bconst], writes=[bp])
                        em.op("act", lambda b=b, pb=pb: nc.scalar.copy(kc_t[:, b * 128:(b + 1) * 128], pb[:, 0:128]),
                              reads=[bp], writes=[bkt])
                    nqb = N // 128
                    for qb in range(nqb):
                        chunks = []
                        if qb > 0:
                            chunks.append((Kr[:, (qb - 1) * 128:qb * 128], V[:, qb - 1, :], 0, bkr, bv))
                        chunks.append((Kr[:, qb * 128:(qb + 1) * 128], V[:, qb, :], None, bkr, bv))
                        if qb < nqb - 1:
                            chunks.append((Kr[:, (qb + 1) * 128:(qb + 2) * 128], V[:, qb + 1, :], 1, bkr, bv))
                        for b in range(c.PAST // 128):
                            chunks.append((kc_t[:, b * 128:(b + 1) * 128], vc[:, b, :], None, bkt, bvc))
                        attend(Qr, bqr, qb, chunks, OUT, bo, kv)
                em.dma("pool", k.MX[kv * 4:kv * 4 + 4, :, t0:t0 + N].rearrange("c p t -> p c t"), OUT[:, :, 0:N],
                       reads=[bo], writes=[k.bMX[i] for i in tis])


def phase_gla(k, l):
    nc, em, c = k.nc, k.em, k.c
    NM = max(c.SEQ, c.DSEQ)
    NCM = NM // 64
    with phase(k):
        wdec = k.sb("gl_wd", [16, 2, c.GW], BF16)
        bwd = Buf("wdec")
        em.dma("pool", wdec[:], k.w_dec[l].rearrange("d r n -> r d n"), writes=[bwd])
        negb = k.sb("gl_nb", [128, 2 * c.GH], F32)
        em.op("dve", lambda: nc.vector.tensor_scalar(negb[:], vcol(k, "bdec%d" % l, 0, 2 * c.GH), -1.0, None, ALU.mult),
              reads=[k.bvec], writes=[bwd])
        gmask = k.sb("gl_gm", [64, 2, 512], BF16)
        em.dma("pool", gmask[:], k.c_gmask, writes=[bwd])
        resetm = k.sb("gl_rm", [128, NM], BF16)
        em.dma("pool", resetm[:], k.c_reset[:, 0:NM], writes=[bwd])

        gd = [Slots(k, "gl_gd%d" % d, [16, NM], BF16, 1) for d in range(2)]
        qs = Slots(k, "gl_q", [128, NM], BF16, 2)
        ks_ = Slots(k, "gl_k", [128, NM], BF16, 2)
        gs = Slots(k, "gl_g", [128, NM], BF16, 1)
        vs = Slots(k, "gl_v", [64, NCM, 128], BF16, 1)
        e1 = Slots(k, "gl_e1", [128, NM], F32, 1)
        l1 = e1
        cum = Slots(k, "gl_cum", [128, NM], F32, 1)
        ecs = Slots(k, "gl_ec", [128, NM], F32, 1)
        dls = Slots(k, "gl_dl", [128, NM], F32, 1)
        ex = Slots(k, "gl_ex", [128, NM], F32, 1)
        Et = [Slots(k, "gl_E%d" % d, [128, NCM], F32, 1) for d in range(2)]
        qin = [Slots(k, "gl_qi%d" % d, [128, NM], BF16, 1) for d in range(2)]
        kin = [Slots(k, "gl_ki%d" % d, [128, NM], BF16, 1) for d in range(2)]
        kdT = Slots(k, "gl_kdT", [128, NM], BF16, 1)
        kd64 = [Slots(k, "gl_kd%d" % d, [64, NCM, 128], BF16, 1) for d in range(2)]
        STf = [Slots(k, "gl_sf%d" % d, [128, NCM + 1, 128], F32, 1) for d in range(2)]
        STb = [Slots(k, "gl_sb%d" % d, [128, NCM, 128], BF16, 1) for d in range(2)]
        atts = [Slots(k, "gl_at%d" % d, [64, 512], BF16, 2) for d in range(2)]
        osq = Slots(k, "gl_osq", [128, 512], BF16, 2)
        rss = Slots(k, "gl_rs", [128, 512], F32, 2)
        sgs = Slots(k, "gl_sg", [128, 512], F32, 2)
        tts = Slots(k, "gl_tt", [128, 512], F32, 2)
        obs = Slots(k, "gl_ob", [128, 512], BF16, 2)

        for (t0, N, lat, si) in c.seqs():
            NCk = N // 64
            G = min(N, 512)
            CG = G // 64
            tis = list(range(t0 // 512, (t0 + N + 511) // 512))
            rPF = [k.bPF[i] for i in tis]
            rPT = [k.bPT[i] for i in tis]
            gdt = []
            for d in range(2):
                g_, bg_ = gd[d].next()
                em.dma("sp", g_[:, 0:N], k.PF[c.f_gd, d * 16:(d + 1) * 16, t0:t0 + N], reads=rPF, writes=[bg_])
                gdt.append((g_, bg_))
            for h in range(c.GH):
                q_, bq = qs.next()
                k_, bk = ks_.next()
                g_, bg = gs.next()
                v_, bv = vs.next()
                em.dma("sp", q_[:, 0:N], k.PF[c.f_gq + h, :, t0:t0 + N], reads=rPF, writes=[bq])
                em.dma("sp", k_[:, 0:N], k.PF[c.f_gk + h, :, t0:t0 + N], reads=rPF, writes=[bk])
                em.dma("sp", g_[:, 0:N], k.PF[c.f_gg + h, :, t0:t0 + N], reads=rPF, writes=[bg])
                em.dma("sp", v_[:, 0:NCk, :], k.PT[t0:t0 + N, c.KW + h * 128:c.KW + (h + 1) * 128]
                       .rearrange("(c s) v -> s c v", s=64), reads=rPT, writes=[bv])
                dd = []
                for d in range(2):
                    gdd, bgd = gdt[d]
                    e1t, be1 = e1.next()
                    for n0 in range(0, N, G):
                        pb, bp = pbank(k)
                        em.op("pe", lambda pb=pb, n0=n0, d=d, gdd=gdd: nc.tensor.matmul(
                            pb[:, 0:G], lhsT=wdec[:, d, h * 128:(h + 1) * 128], rhs=gdd[:, n0:n0 + G],
                            start=True, stop=True), reads=[bgd, bwd], writes=[bp])
                        em.op("act", lambda pb=pb, n0=n0, d=d: nc.scalar.activation(
                            e1t[:, n0:n0 + G], pb[:, 0:G], AF.Exp, bias=negb[:, d * c.GH + h:d * c.GH + h + 1], scale=-1.0),
                            reads=[bp, bwd], writes=[be1])
                    l1t, bl1 = l1.next()
                    em.op("act", lambda: nc.scalar.activation(l1t[:, 0:N], e1t[:, 0:N], AF.Ln, bias=k.onet[:], scale=1.0),
                          reads=[be1, k.bconst], writes=[bl1])
                    cm, bcm = cum.next()
                    em.op("dve", lambda: nc.vector.tensor_tensor_scan(cm[:, 0:N], resetm[:, 0:N], l1t[:, 0:N], 0.0,
                                                                      ALU.mult, ALU.add), reads=[bl1, bwd], writes=[bcm])
                    tot = mk(cm[:], 63, [[64, NCk]])
                    totb = bc_last(tot, 64)
                    cm3 = cm[:, 0:N].rearrange("p (c s) -> p c s", s=64)
                    l13 = l1t[:, 0:N].rearrange("p (c s) -> p c s", s=64)
                    dl, bdl = dls.next()
                    dl3 = dl[:, 0:N].rearrange("p (c s) -> p c s", s=64)
                    if d == 0:
                        ec, bec = cm, bcm
                        em.op("dve", lambda: nc.vector.tensor_tensor(dl3, cm3, totb, ALU.subtract),
                              reads=[bcm], writes=[bdl])
                    else:
                        ec, bec = ecs.next()
                        ec3 = ec[:, 0:N].rearrange("p (c s) -> p c s", s=64)
                        em.op("dve", lambda: nc.vector.tensor_tensor(ec3, totb, cm3, ALU.subtract),
                              reads=[bcm], writes=[bec])
                        em.op("pool", lambda: nc.gpsimd.tensor_tensor(ec[:, 0:N], ec[:, 0:N], l1t[:, 0:N], ALU.add),
                              reads=[bec, bl1], writes=[bec])
                        em.op("pool", lambda: nc.gpsimd.tensor_tensor(dl[:, 0:N], l1t[:, 0:N], cm[:, 0:N], ALU.subtract),
                              reads=[bl1, bcm], writes=[bdl])
                    E, bE = Et[d].next()
                    em.op("act", lambda: nc.scalar.activation(E[:, 0:NCk], tot, AF.Exp, scale=-1.0 / 16),
                          reads=[bcm], writes=[bE])
                    qi, bqi = qin[d].next()
                    ki, bki = kin[d].next()
                    kd, bkd = kdT.next()
                    x1, bx1 = ex.next()
                    em.op("act", lambda: nc.scalar.activation(x1[:, 0:N], ec[:, 0:N], AF.Exp, scale=-1.0 / 16),
                          reads=[bec], writes=[bx1])
                    em.op("dve", lambda: nc.vector.scalar_tensor_tensor(qi[:, 0:N], q_[:, 0:N], 128.0 ** -0.5, x1[:, 0:N],
                                                                        ALU.mult, ALU.mult), reads=[bq, bx1], writes=[bqi])
                    x2, bx2 = ex.next()
                    em.op("act", lambda: nc.scalar.activation(x2[:, 0:N], ec[:, 0:N], AF.Exp, scale=1.0 / 16),
                          reads=[bec], writes=[bx2])
                    em.op("pool", lambda: nc.gpsimd.tensor_tensor(ki[:, 0:N], k_[:, 0:N], x2[:, 0:N], ALU.mult),
                          reads=[bk, bx2], writes=[bki])
                    x3, bx3 = ex.next()
                    em.op("act", lambda: nc.scalar.activation(x3[:, 0:N], dl[:, 0:N], AF.Exp, scale=1.0 / 16),
                          reads=[bdl], writes=[bx3])
                    em.op("dve", lambda: nc.vector.tensor_tensor(kd[:, 0:N], k_[:, 0:N], x3[:, 0:N], ALU.mult),
                          reads=[bk, bx3], writes=[bkd])
                    k64, bk64 = kd64[d].next()
                    for n0 in range(0, N, G):
                        pb, bp = pbank(k)
                        pbb = pb[:].bitcast(BF16)
                        em.group("pe", [
                            (lambda ci=ci, pbb=pbb, n0=n0: nc.tensor.transpose(
                                pbb[0:64, ci * 128:(ci + 1) * 128], kd[:, n0 + ci * 64:n0 + (ci + 1) * 64], k.identb[:]))
                            for ci in range(CG)], reads=[bkd, k.bconst], writes=[bp])
                        c0 = n0 // 64
                        em.op("act", lambda pbb=pbb, c0=c0: nc.scalar.copy(
                            k64[:, c0:c0 + CG, :], pbb[0:64, 0:CG * 128].rearrange("p (c d) -> p c d", d=128)),
                            reads=[bp], writes=[bk64])
                    sf, bsf = STf[d].next()
                    if lat:
                        em.dma("sp", sf[:, 0, :], k.sg[l, d, h], writes=[bsf])
                    else:
                        em.op("pool", lambda: nc.gpsimd.memset(sf[:, 0, :], 0.0), writes=[bsf])
                    order = list(range(NCk)) if d == 0 else list(range(NCk - 1, -1, -1))
                    for g0 in range(0, NCk, 4):
                        pb, bp = pbank(k)
                        em.group("pe", [
                            (lambda j=j, cc=order[g0 + j], pb=pb: nc.tensor.matmul(
                                pb[:, j * 128:(j + 1) * 128], lhsT=k64[:, cc, :], rhs=v_[:, cc, :], start=True, stop=True))
                            for j in range(min(4, NCk - g0))], reads=[bk64, bv], writes=[bp])
                        for j in range(min(4, NCk - g0)):
                            idx = g0 + j
                            cc = order[idx]
                            em.op("dve", lambda idx=idx, cc=cc, j=j, pb=pb: nc.vector.scalar_tensor_tensor(
                                sf[:, idx + 1, :], sf[:, idx, :], E[:, cc:cc + 1], pb[:, j * 128:(j + 1) * 128],
                                ALU.mult, ALU.add), reads=[bsf, bE, bp], writes=[bsf])
                    sb_, bsb = STb[d].next()
                    em.op("act", lambda: nc.scalar.copy(sb_[:, 0:NCk, :], sf[:, 0:NCk, :]), reads=[bsf], writes=[bsb])
                    if not lat:
                        em.dma("pool", k.ngla[si, l, d, h], sf[:, NCk, :], reads=[bsf])
                    dd.append((qi, bqi, ki, bki, sb_, bsb))
                for n0 in range(0, N, G):
                    c0 = n0 // 64
                    att = []
                    for d in range(2):
                        qi, bqi, ki, bki, sb_, bsb = dd[d]
                        pa, bpa = pbank(k)
                        em.group("pe", [
                            (lambda ci=ci, pa=pa, qi=qi, ki=ki: nc.tensor.matmul(
                                pa[0:64, ci * 64:(ci + 1) * 64], lhsT=ki[:, n0 + ci * 64:n0 + (ci + 1) * 64],
                                rhs=qi[:, n0 + ci * 64:n0 + (ci + 1) * 64], start=True, stop=True))
                            for ci in range(CG)], reads=[bqi, bki], writes=[bpa])
                        at_, bat = atts[d].next()
                        em.op("dve", lambda pa=pa, at_=at_, d=d: nc.vector.tensor_tensor(
                            at_[:, 0:G], pa[0:64, 0:G], gmask[:, d, 0:G], ALU.mult), reads=[bpa, bwd], writes=[bat])
                        att.append((at_, bat))
                    po, bpo = pbank(k)
                    fns = []
                    for ci in range(CG):
                        cc = c0 + ci
                        sl = slice(ci * 64, (ci + 1) * 64)
                        tsl = slice(n0 + ci * 64, n0 + (ci + 1) * 64)
                        fns.append(lambda sl=sl, cc=cc: nc.tensor.matmul(po[:, sl], lhsT=v_[:, cc, :], rhs=att[0][0][:, sl],
                                                                        start=True, stop=False))
                        fns.append(lambda sl=sl, cc=cc, tsl=tsl: nc.tensor.matmul(
                            po[:, sl], lhsT=dd[0][4][:, cc, :], rhs=dd[0][0][:, tsl], start=False, stop=False))
                        fns.append(lambda sl=sl, cc=cc: nc.tensor.matmul(po[:, sl], lhsT=v_[:, cc, :], rhs=att[1][0][:, sl],
                                                                        start=False, stop=False))
                        fns.append(lambda sl=sl, cc=cc, tsl=tsl: nc.tensor.matmul(
                            po[:, sl], lhsT=dd[1][4][:, NCk - 1 - cc, :], rhs=dd[1][0][:, tsl], start=False, stop=True))
                    em.group("pe", fns, reads=[bv, att[0][1], att[1][1], dd[0][1], dd[1][1], dd[0][5], dd[1][5]],
                             writes=[bpo])
                    oq, boq = osq.next()
                    em.op("act", lambda: nc.scalar.activation(oq[:, 0:G], po[:, 0:G], AF.Square), reads=[bpo], writes=[boq])
                    pn, bpn = pbank(k)
                    em.op("pe", lambda: nc.tensor.matmul(pn[:, 0:G], lhsT=k.onesb[:], rhs=oq[:, 0:G], start=True, stop=True),
                          reads=[boq, k.bconst], writes=[bpn])
                    rs, brs = rss.next()
                    em.op("act", lambda: nc.scalar.activation(rs[:, 0:G], pn[:, 0:G], AF.Ln, bias=k.epst[:], scale=1.0 / 128),
                          reads=[bpn, k.bconst], writes=[brs])
                    em.op("act", lambda: nc.scalar.activation(rs[:, 0:G], rs[:, 0:G], AF.Exp, scale=-0.5),
                          reads=[brs], writes=[brs])
                    sg_, bsg = sgs.next()
                    em.op("act", lambda: nc.scalar.activation(sg_[:, 0:G], g_[:, n0:n0 + G], AF.Silu), reads=[bg], writes=[bsg])
                    tt, btt = tts.next()
                    em.op("dve", lambda: nc.vector.scalar_tensor_tensor(
                        tt[:, 0:G], po[:, 0:G], vcol(k, "gnorm%d" % l, h, 1), rs[:, 0:G], ALU.mult, ALU.mult),
                        reads=[bpo, brs, k.bvec], writes=[btt])
                    ob, bob = obs.next()
                    em.op("pool", lambda: nc.gpsimd.tensor_tensor(ob[:, 0:G], tt[:, 0:G], sg_[:, 0:G], ALU.mult),
                          reads=[btt, bsg], writes=[bob])
                    em.dma("pool", k.MX[c.AH + h, :, t0 + n0:t0 + n0 + G], ob[:, 0:G], reads=[bob],
                           writes=[k.bMX[(t0 + n0) // 512]])


def phase_lru(k, l):
    nc, em, c = k.nc, k.em, k.c
    NM = max(c.SEQ, c.DSEQ)
    LB = c.LB
    with phase(k):
        wr = k.sb("lr_wr", [128, 2 * LB, 128], BF16)
        wi = k.sb("lr_wi", [128, 2 * LB, 128], BF16)
        bw = Buf("lruw")
        em.dma("pool", wr[:], k.w_r[l].rearrange("d b i j -> i (d b) j"), writes=[bw])
        em.dma("pool", wi[:], k.w_i[l].rearrange("d b i j -> i (d b) j"), writes=[bw])
        cl = k.sb("lr_cl", [128, 2 * LB], F32)
        em.op("act", lambda: nc.scalar.activation(cl[:], vcol(k, "lam%d" % l, 0, 2 * LB), AF.Exp, scale=-1.0),
              reads=[k.bvec], writes=[bw])
        em.op("act", lambda: nc.scalar.activation(cl[:], cl[:], AF.Ln, bias=k.onet[:], scale=1.0),
              reads=[bw, k.bconst], writes=[bw])
        em.op("dve", lambda: nc.vector.tensor_scalar(cl[:], cl[:], -8.0, None, ALU.mult), reads=[bw], writes=[bw])
        fs = k.sb("lr_fs", [128, c.NSEQ * 2 * LB], F32)
        bfs = Buf("fs")
        xs = Slots(k, "lr_x", [128, NM], BF16, 2)
        ys = Slots(k, "lr_y", [128, NM], BF16, 2)
        xcs = Slots(k, "lr_xc", [128, NM], F32, 1)
        xbs = Slots(k, "lr_xb", [128, NM], BF16, 1)
        rs_ = Slots(k, "lr_r", [128, NM], F32, 1)
        is_ = Slots(k, "lr_i", [128, NM], F32, 1)
        as_ = Slots(k, "lr_a", [128, NM], F32, 1)
        us_ = Slots(k, "lr_u", [128, NM], F32, 1)
        hs_ = [Slots(k, "lr_h%d" % d, [128, NM], F32, 1) for d in range(2)]
        gys = Slots(k, "lr_gy", [128, NM], F32, 1)
        obs = Slots(k, "lr_ob", [128, NM], BF16, 2)
        for (t0, N, lat, si) in c.seqs():
            tis = list(range(t0 // 512, (t0 + N + 511) // 512))
            rPF = [k.bPF[i] for i in tis]
            for j in range(LB):
                x_, bx = xs.next()
                y_, by = ys.next()
                em.dma("sp", x_[:, 0:N], k.PF[c.f_lx + j, :, t0:t0 + N], reads=rPF, writes=[bx])
                em.dma("sp", y_[:, 0:N], k.PF[c.f_ly + j, :, t0:t0 + N], reads=rPF, writes=[by])
                xc, bxc = xcs.next()
                cw = lambda tap: vcol(k, "convw%d" % l, tap * LB + j, 1)
                em.op("dve", lambda: nc.vector.tensor_scalar(xc[:, 0:N], x_[:, 0:N], cw(2), vcol(k, "convb%d" % l, j, 1),
                                                             ALU.mult, ALU.add), reads=[bx, k.bvec], writes=[bxc])
                em.op("dve", lambda: nc.vector.scalar_tensor_tensor(xc[:, 2:N], x_[:, 0:N - 2], cw(0), xc[:, 2:N],
                                                                    ALU.mult, ALU.add), reads=[bx, bxc, k.bvec], writes=[bxc])
                em.op("dve", lambda: nc.vector.scalar_tensor_tensor(xc[:, 1:N], x_[:, 0:N - 1], cw(1), xc[:, 1:N],
                                                                    ALU.mult, ALU.add), reads=[bx, bxc, k.bvec], writes=[bxc])
                em.op("dve", lambda: nc.vector.scalar_tensor_tensor(xc[:, 0:N - 1], x_[:, 1:N], cw(3), xc[:, 0:N - 1],
                                                                    ALU.mult, ALU.add), reads=[bx, bxc, k.bvec], writes=[bxc])
                xb, bxb = xbs.next()
                em.op("act", lambda: nc.scalar.copy(xb[:, 0:N], xc[:, 0:N]), reads=[bxc], writes=[bxb])
                hh = []
                for d in range(2):
                    r_, br_ = rs_.next()
                    i_, bi_ = is_.next()
                    G = min(N, 512)
                    for n0 in range(0, N, G):
                        for (wt, dst, bdst, bname) in ((wr, r_, br_, "br%d"), (wi, i_, bi_, "bi%d")):
                            pb, bp = pbank(k)
                            em.op("pe", lambda pb=pb, wt=wt, n0=n0: nc.tensor.matmul(
                                pb[:, 0:G], lhsT=wt[:, d * LB + j, :], rhs=xb[:, n0:n0 + G], start=True, stop=True),
                                reads=[bw, bxb], writes=[bp])
                            em.op("act", lambda pb=pb, dst=dst, n0=n0, bname=bname: nc.scalar.activation(
                                dst[:, n0:n0 + G], pb[:, 0:G], AF.Sigmoid, bias=vcol(k, bname % l, d * LB + j, 1), scale=1.0),
                                reads=[bp, k.bvec], writes=[bdst])
                    a_, ba_ = as_.next()
                    u_, bu_ = us_.next()
                    em.op("act", lambda: nc.scalar.activation(a_[:, 0:N], r_[:, 0:N], AF.Exp,
                                                              scale=cl[:, d * LB + j:d * LB + j + 1]),
                          reads=[br_, bw], writes=[ba_])
                    em.op("pool", lambda: nc.gpsimd.tensor_tensor(u_[:, 0:N], a_[:, 0:N], a_[:, 0:N], ALU.mult),
                          reads=[ba_], writes=[bu_])
                    em.op("act", lambda: nc.scalar.activation(u_[:, 0:N], u_[:, 0:N], AF.Sqrt, bias=k.onet[:], scale=-1.0),
                          reads=[bu_, k.bconst], writes=[bu_])
                    em.op("pool", lambda: nc.gpsimd.tensor_tensor(i_[:, 0:N], i_[:, 0:N], xc[:, 0:N], ALU.mult),
                          reads=[bi_, bxc], writes=[bi_])
                    em.op("dve", lambda: nc.vector.tensor_tensor(u_[:, 0:N], u_[:, 0:N], i_[:, 0:N], ALU.mult),
                          reads=[bu_, bi_], writes=[bu_])
                    h_, bh_ = hs_[d].next()
                    init = vcol(k, "lru0%d" % l, d * LB + j, 1) if lat else 0.0
                    if d == 0:
                        em.op("dve", lambda: nc.vector.tensor_tensor_scan(h_[:, 0:N], a_[:, 0:N], u_[:, 0:N], init,
                                                                          ALU.mult, ALU.add),
                              reads=[ba_, bu_, k.bvec], writes=[bh_])
                    else:
                        em.op("dve", lambda: nc.vector.tensor_tensor_scan(rev(h_[:, 0:N]), rev(a_[:, 0:N]), rev(u_[:, 0:N]),
                                                                          init, ALU.mult, ALU.add),
                              reads=[ba_, bu_, k.bvec], writes=[bh_])
                    if not lat:
                        col = (si * 2 + d) * LB + j
                        src = h_[:, N - 1:N] if d == 0 else h_[:, 0:1]
                        em.op("pool", lambda src=src, col=col: nc.gpsimd.tensor_copy(fs[:, col:col + 1], src),
                              reads=[bh_], writes=[bfs])
                    hh.append((h_, bh_))
                gy, bgy = gys.next()
                em.op("act", lambda: nc.scalar.activation(gy[:, 0:N], y_[:, 0:N], AF.Gelu_apprx_tanh), reads=[by], writes=[bgy])
                em.op("pool", lambda: nc.gpsimd.tensor_tensor(hh[0][0][:, 0:N], hh[0][0][:, 0:N], hh[1][0][:, 0:N], ALU.add),
                      reads=[hh[0][1], hh[1][1]], writes=[hh[0][1]])
                ob, bob = obs.next()
                em.op("dve", lambda: nc.vector.tensor_tensor(ob[:, 0:N], hh[0][0][:, 0:N], gy[:, 0:N], ALU.mult),
                      reads=[hh[0][1], bgy], writes=[bob])
                em.dma("pool", k.MX[c.AH + c.GH + j, :, t0:t0 + N], ob[:, 0:N], reads=[bob],
                       writes=[k.bMX[i] for i in tis])
        ncol = c.NSEQ * 2 * LB
        pb, bp = pbank(k)
        em.op("pe", lambda: nc.tensor.transpose(pb[0:ncol, 0:128], fs[:, 0:ncol], k.identf[:]),
              reads=[bfs, k.bconst], writes=[bp])
        fso = k.sb("lr_fso", [128, 128], F32)
        bfo = Buf("fso")
        em.op("dve", lambda: nc.vector.tensor_copy(fso[0:ncol, :], pb[0:ncol, 0:128]), reads=[bp], writes=[bfo])
        for si in range(c.NSEQ):
            em.dma("pool", k.nlru[si, l].rearrange("d (j p) -> (d j) p", p=128), fso[si * 2 * LB:(si + 1) * 2 * LB, :],
                   reads=[bfo])


def resid_gemm(k, l, src, bsrc, KCin, W, gate_off, wcols, nslots_a, pfx, k0=0):
    nc, em, c = k.nc, k.em, k.c
    src = src[k0:k0 + KCin]
    W = W[k0 * 128:(k0 + KCin) * 128, :]
    with phase(k):
        acts = Slots(k, pfx + "a", [128, KCin, 512], BF16, nslots_a)
        ws = Slots(k, pfx + "w", [128, KCin, wcols], BF16, 3)
        xs = Slots(k, pfx + "x", [128, wcols // 128, 512], F32, 3)
        for (t0s, r) in c.tile_pairs(nslots_a):
            ats = [load_act_tile(k, acts, src, bsrc, KCin, t0) for t0 in t0s]
            for g0 in range(0, c.D, wcols):
                wt, bw = load_w(k, ws, W, KCin, g0, wcols)
                nj = wcols // 128
                m0 = g0 // 128
                for t0, (at, ba) in zip(t0s, ats):
                    ti = t0 // 512
                    xt, bx = xs.next()
                    em.dma("sp", xt[:], k.XT[m0:m0 + nj, :, t0:t0 + 512].rearrange("c p t -> p c t"),
                           reads=[k.bXT[ti]], writes=[bx])
                    for j in range(nj):
                        pb, bp = mm_fm(k, wt, bw, at, ba, KCin, j)
                        gate = k.mod[:, l, gate_off + m0 + j, r:r + 1]
                        em.op("dve", lambda pb=pb, j=j, gate=gate, xt=xt: nc.vector.scalar_tensor_tensor(
                            xt[:, j, :], pb[:], gate, xt[:, j, :], ALU.mult, ALU.add), reads=[bp, bx, k.bmod], writes=[bx])
                    em.dma("sp", k.XT[m0:m0 + nj, :, t0:t0 + 512].rearrange("c p t -> p c t"), xt[:],
                           reads=[bx], writes=[k.bXT[ti]])


def phase_outproj(k, l):
    resid_gemm(k, l, k.MX, k.bMX, k.c.KCM, k.w_out[l], 2 * k.c.KC, min(512, k.c.D), 2, "op_")


def phase_down(k, l):
    FC = k.c.FC
    h = (FC + 1) // 2
    if FC <= 48:
        resid_gemm(k, l, k.AC, k.bAC, FC, k.w_down[l], 5 * k.c.KC, min(256, k.c.D), 2, "dn_")
    else:
        resid_gemm(k, l, k.AC, k.bAC, h, k.w_down[l], 5 * k.c.KC, 256, 2, "dna_", k0=0)
        resid_gemm(k, l, k.AC, k.bAC, FC - h, k.w_down[l], 5 * k.c.KC, 256, 2, "dnb_", k0=h)


def phase_gu(k, l):
    nc, em, c = k.nc, k.em, k.c
    KC = c.KC
    W = k.w_gu[l]
    with phase(k):
        acts = Slots(k, "gu_a", [128, KC, 512], BF16, 2)
        GWD = 256
        wg = Slots(k, "gu_g", [128, KC, GWD], BF16, 3)
        wu = Slots(k, "gu_u", [128, KC, GWD], BF16, 3)
        sgs = Slots(k, "gu_s", [128, 512], F32, 3)
        stg = Slots(k, "gu_o", [128, GWD // 128, 512], BF16, 3)
        bg = None
        if l + 1 < c.L:
            k.nbanks = 7
            bg = ada_bg(k, l + 1, Slots(k, "gu_ada", [128, KC, 128], BF16, 3), k.pb[7], k.bpb[7])
            nsteps = len(c.tile_pairs()) * ((c.DFF + GWD - 1) // GWD)
            per = -(-(6 * KC) // nsteps)
        for (t0s, r) in c.tile_pairs():
            ats = [load_act_tile(k, acts, k.HN, k.bHN, KC, t0) for t0 in t0s]
            for f0 in range(0, c.DFF, GWD):
                if bg is not None:
                    for _ in range(per):
                        next(bg, None)
                n = min(GWD, c.DFF - f0)
                nj = n // 128
                wgt, bwg = load_w(k, wg, W, KC, f0, n)
                wut, bwu = load_w(k, wu, W, KC, c.DFF + f0, n)
                for t0, (at, ba) in zip(t0s, ats):
                    ti = t0 // 512
                    so, bso = stg.next()
                    for j in range(nj):
                        pg, bpg = mm_fm(k, wgt, bwg, at, ba, KC, j)
                        pu, bpu = mm_fm(k, wut, bwu, at, ba, KC, j)
                        sg_, bsg = sgs.next()
                        em.op("act", lambda pg=pg, sg_=sg_: nc.scalar.activation(sg_[:], pg[:], AF.Silu), reads=[bpg], writes=[bsg])
                        em.op("dve", lambda pu=pu, sg_=sg_, j=j, so=so: nc.vector.tensor_tensor(so[:, j, :], sg_[:], pu[:], ALU.mult),
                              reads=[bsg, bpu], writes=[bso])
                    em.dma("sp", k.AC[f0 // 128:f0 // 128 + nj, :, t0:t0 + 512].rearrange("c p t -> p c t"), so[:, 0:nj, :],
                           reads=[bso], writes=[k.bAC[ti]])
        if bg is not None:
            for _ in bg:
                pass
            k.nbanks = 8


def phase_final(k):
    nc, em, c = k.nc, k.em, k.c
    KC = c.KC
    with phase(k):
        xs = Slots(k, "fn_x", [128, KC, 512], F32, 2)
        sqs = Slots(k, "fn_q", [128, KC, 512], BF16, 1)
        rss = Slots(k, "fn_r", [128, 512], F32, 2)
        ys = Slots(k, "fn_y", [128, c.D], F32, 2)
        gb = bc_last(vcol(k, "fng", 0, KC), 512)
        ei = 0
        for (t0, r) in c.tiles():
            ti = t0 // 512
            xt, bx = xs.next()
            em.dma("sp", xt[:], k.XT[:, :, t0:t0 + 512].rearrange("c p t -> p c t"), reads=[k.bXT[ti]], writes=[bx])
            sq, bsq = sqs.next()
            rs, brs = rss.next()
            norm_block(k, xt, bx, 512, sq, bsq, rs, brs)
            em.op("dve", lambda: nc.vector.tensor_tensor(xt[:], xt[:], bc_mid(rs[:], KC), ALU.mult),
                  reads=[bx, brs], writes=[bx])
            em.op("dve", lambda: nc.vector.tensor_tensor(xt[:], xt[:], gb, ALU.mult), reads=[bx, k.bvec], writes=[bx])
            for blk in range(4):
                yt, by = ys.next()
                for c0 in range(0, KC, 4):
                    n = min(4, KC - c0)
                    pb, bp = pbank(k)
                    em.group("pe", [
                        (lambda j=j, pb=pb: nc.tensor.transpose(pb[:, j * 128:(j + 1) * 128],
                                                                xt[:, c0 + j, blk * 128:(blk + 1) * 128], k.identf[:]))
                        for j in range(n)], reads=[bx, k.bconst], writes=[bp])
                    evac(k, ei, yt[:, c0 * 128:(c0 + n) * 128], pb[:, 0:n * 128], [bp], [by])
                    ei += 1
                tok = t0 + blk * 128
                em.dma("pool", k.y[tok:tok + 128, :], yt[:], reads=[by])


def host_consts(c):
    ident = np.eye(128, dtype=np.float32)
    pm = np.zeros((128, 128), np.float32)
    for p in range(128):
        j = p % 64
        partner = p + 32 if j < 32 else p - 32
        pm[partner, p] = 1.0
    GRID_W = 64
    rows = c.DSEQ // GRID_W
    row = np.repeat(np.arange(rows, dtype=np.float32), GRID_W)
    col = np.tile(np.arange(GRID_W, dtype=np.float32), rows)
    inv = (np.float32(10000.0) ** (-np.arange(32, dtype=np.float32) / np.float32(32))).astype(np.float32)
    rope = np.zeros((128, 2, c.DSEQ), np.float32)
    for p in range(128):
        half, j = p // 64, p % 64
        pos = row if half == 0 else col
        i = j % 32
        ang = (pos * inv[i]).astype(np.float32)
        rope[p, 0] = np.cos(ang)
        rope[p, 1] = -np.sin(ang) if j < 32 else np.sin(ang)
    kj = np.arange(128)[:, None]
    qi = np.arange(128)[None, :]
    mL = np.where(kj >= qi, 0.0, -30000.0).astype(np.float32)
    mR = np.where(kj <= qi, 0.0, -30000.0).astype(np.float32)
    mlr = np.stack([np.tile(mL, (1, 4)), np.tile(mR, (1, 4))], axis=1)
    s_ = np.arange(64)[:, None]
    t_ = np.arange(64)[None, :]
    gF = (s_ <= t_).astype(np.float32)
    gB = (s_ >= t_).astype(np.float32)
    gmask = np.stack([np.tile(gF, (1, 8)), np.tile(gB, (1, 8))], axis=1)
    reset = np.ones((128, 2048), np.float32)
    reset[:, 0::64] = 0.0
    return dict(c_ident=ident, c_pm=pm, c_rope=rope, c_mlr=np.ascontiguousarray(mlr),
                c_gmask=np.ascontiguousarray(gmask), c_reset=reset)


def host_vecs(c, inp, core):
    v = np.zeros((128, c.NV), np.float32)

    def put(name, arr2d):
        o = c.vcol[name]
        v[:, o:o + arr2d.shape[1]] = arr2d
    cT = np.stack([feat_major(inp["c_ctx"]), feat_major(inp["c"][core])], axis=2)
    put("cT", cT.reshape(128, -1))
    put("fng", feat_major(inp["final_norm_g"]))
    for l in range(c.L):
        put("bada%d" % l, feat_major(inp["b_ada"][l]))
        put("n1g%d" % l, feat_major(inp["norm1_g"][l]))
        put("n2g%d" % l, feat_major(inp["norm2_g"][l]))
        put("sink%d" % l, np.broadcast_to(np.asarray(inp["attn_sink"][l], np.float32)[None, :], (128, c.AH)))
        put("bdec%d" % l, np.concatenate([feat_major(inp["gla_b_decay"][l, d]) for d in range(2)], axis=1))
        put("gnorm%d" % l, feat_major(inp["gla_norm_g"][l]))
        put("convw%d" % l, np.concatenate([feat_major(inp["lru_conv_w"][l, t]) for t in range(4)], axis=1))
        put("convb%d" % l, feat_major(inp["lru_conv_b"][l]))
        put("br%d" % l, np.concatenate([feat_major(inp["lru_b_rgate"][l, d]) for d in range(2)], axis=1))
        put("bi%d" % l, np.concatenate([feat_major(inp["lru_b_igate"][l, d]) for d in range(2)], axis=1))
        put("lam%d" % l, np.concatenate([feat_major(inp["lru_lambda"][l, d]) for d in range(2)], axis=1))
        put("lru0%d" % l, np.concatenate([feat_major(inp["state_lru"][core, l, d]) for d in range(2)], axis=1))
    return v


_NC_CACHE = {}


def run(c, inp, n_cores):
    key = (c.D, c.AH, c.KV, c.GH, c.LB, c.DFF, c.SEQ, c.NSEQ, c.DSEQ, c.PAST, c.L)
    if key not in _NC_CACHE:
        _NC_CACHE[key] = build(c)
    nc = _NC_CACHE[key]
    f32 = lambda a: np.ascontiguousarray(np.asarray(a, np.float32))
    consts = host_consts(c)
    shared = dict(
        w_ada=f32(inp["w_ada"]), w_in=f32(inp["w_in"]), w_dec=f32(inp["gla_w_decay"]),
        w_r=f32(inp["lru_w_rgate"]), w_i=f32(inp["lru_w_igate"]), w_out=f32(inp["w_out"]),
        w_gu=f32(inp["w_gu"]), w_down=f32(inp["w_down"]), **consts)
    xp = f32(inp["x_prompt"])
    xs = f32(inp["x_sample"])
    in_maps = []
    for i in range(n_cores):
        m = dict(shared)
        m["xin"] = np.concatenate([xp[i * c.NSEQ:(i + 1) * c.NSEQ].reshape(c.NCT, c.D), xs[i]], axis=0)
        m["ck"] = f32(inp["cache_attn_k"][i]).reshape(c.L, c.PAST, c.KW)
        m["cv"] = f32(inp["cache_attn_v"][i]).reshape(c.L, c.PAST, c.KW)
        m["sg"] = f32(inp["state_gla"][i])
        m["vecs"] = host_vecs(c, inp, i)
        in_maps.append(m)
    res = run_bass_kernel_spmd(nc, in_maps, core_ids=list(range(n_cores))).results
    B = n_cores * c.NSEQ
    y_prompt = np.concatenate([r["y"][:c.NCT].reshape(c.NSEQ, c.SEQ, c.D) for r in res], axis=0)
    y_sample = np.stack([r["y"][c.NCT:] for r in res], axis=0)
    nk = np.concatenate([r["nk"].reshape(c.NSEQ, c.L, c.SEQ, c.KV, 128) for r in res], axis=0)
    nv = np.concatenate([r["nv"].reshape(c.NSEQ, c.L, c.SEQ, c.KV, 128) for r in res], axis=0)
    ngla = np.concatenate([r["ngla"] for r in res], axis=0)
    nlru = np.concatenate([r["nlru"] for r in res], axis=0)
    return (y_prompt.astype(np.float32), y_sample.astype(np.float32), nk.astype(np.float32),
            nv.astype(np.float32), ngla.astype(np.float32), nlru.astype(np.float32))


def kernel(**inputs):
    return run(Cfg(), inputs, 8)
```

```python
import contextlib
import math
import numpy as np
import concourse.bass as bass
import concourse.mybir as mybir
from concourse.ap import AP
from concourse.bass_utils import run_bass_kernel_spmd

F32 = mybir.dt.float32
BF16 = mybir.dt.bfloat16
ALU = mybir.AluOpType
AF = mybir.ActivationFunctionType
EPS = 1e-6
NPHASES = 1000
SERIAL = 0
PESER = 0
NDMA_SP = 4
SERSET = ()
FV = 0


class Buf:
    __slots__ = ("name", "w", "r", "excl")

    def __init__(self, name="", excl=False):
        self.name = name
        self.w = None
        self.r = []
        self.excl = excl


class Em:
    NDMA = 3

    def __init__(self, nc, stack):
        self.nc = nc
        self.engs = {"pe": nc.tensor, "act": nc.scalar, "dve": nc.vector,
                     "pool": nc.gpsimd, "sp": nc.sync}
        self.sems = {}
        self.cnt = {}
        for k in self.engs:
            self.sems[k] = stack.enter_context(nc.semaphore("s_" + k))
            self.cnt[k] = 0
        self.dq = {}
        for q in ("sp", "pool", "act"):
            lst = []
            for i in range(self.NDMA if q != "sp" else NDMA_SP):
                key = "d_%s%d" % (q, i)
                self.sems[key] = stack.enter_context(nc.semaphore(key))
                self.cnt[key] = 0
                lst.append(key)
            self.dq[q] = [lst, 0]
        self.known = {k: {} for k in self.engs}
        self.n_ins = 0
        for key in self.sems:
            nc.gpsimd.sem_clear(self.sems[key])
        nc.all_engine_barrier()

    def finish(self):
        self.barrier()
        self.nc.all_engine_barrier()
        for key in self.sems:
            self.nc.gpsimd.sem_clear(self.sems[key])
        self.nc.all_engine_barrier()

    def _wait(self, e, evs):
        need = {}
        for ev in evs:
            if ev is None:
                continue
            k, v = ev
            if v > need.get(k, 0):
                need[k] = v
        kn = self.known[e]
        for k, v in need.items():
            if kn.get(k, 0) >= v:
                continue
            self.engs[e].wait_ge(self.sems[k], v)
            kn[k] = v
            self.n_ins += 1

    @staticmethod
    def _deps(reads, writes):
        evs = []
        for b in reads:
            evs.append(b.w)
            if b.excl:
                evs.extend(b.r)
        for b in writes:
            evs.append(b.w)
            evs.extend(b.r)
        return evs

    @staticmethod
    def _commit(ev, reads, writes):
        for b in reads:
            b.r.append(ev)
            if len(b.r) > 48:
                best = {}
                for k, v in b.r:
                    if v > best.get(k, 0):
                        best[k] = v
                b.r = list(best.items())
        for b in writes:
            b.w = ev
            b.r = []

    def op(self, e, fn, reads=(), writes=()):
        if PESER and e == "pe" and self.cnt["pe"] > 0:
            self._wait("pe", [("pe", self.cnt["pe"])])
        self._wait(e, self._deps(reads, writes))
        ins = fn()
        self.cnt[e] += 1
        ins.then_inc(self.sems[e], 1)
        ev = (e, self.cnt[e])
        self._commit(ev, reads, writes)
        self.n_ins += 1
        if SERIAL or e in SERSET:
            self.barrier()
        return ev

    def group(self, e, fns, reads=(), writes=()):
        if PESER and e == "pe" and self.cnt["pe"] > 0:
            self._wait("pe", [("pe", self.cnt["pe"])])
        self._wait(e, self._deps(reads, writes))
        ins = None
        for fn in fns:
            ins = fn()
            self.n_ins += 1
        self.cnt[e] += 1
        ins.then_inc(self.sems[e], 1)
        ev = (e, self.cnt[e])
        self._commit(ev, reads, writes)
        if SERIAL or e in SERSET:
            self.barrier()
        return ev

    def dma(self, q, out, in_, reads=(), writes=(), **kw):
        lst, idx = self.dq[q]
        key = lst[idx % len(lst)]
        self.dq[q][1] = idx + 1
        evs = self._deps(reads, writes)
        if self.cnt[key] > 0:
            evs.append((key, self.cnt[key]))
        self._wait(q, evs)
        ins = self.engs[q].dma_start(out=out, in_=in_, **kw)
        self.cnt[key] += 16
        ins.then_inc(self.sems[key], 16)
        ev = (key, self.cnt[key])
        self._commit(ev, reads, writes)
        self.n_ins += 1
        if SERIAL or ("dma_" + q) in SERSET:
            self.barrier()
        return ev

    def barrier(self):
        evs = [(k, v) for k, v in self.cnt.items() if v > 0]
        for e in self.engs:
            self._wait(e, evs)


class Cfg:
    def __init__(self, D=4096, AH=16, KV=4, GH=8, LB=8, DFF=11008, SEQ=256, NSEQ=4,
                 DSEQ=2048, PAST=256, L=2):
        self.D, self.AH, self.KV, self.GH, self.LB, self.DFF = D, AH, KV, GH, LB, DFF
        self.SEQ, self.NSEQ, self.DSEQ, self.PAST, self.L = SEQ, NSEQ, DSEQ, PAST, L
        self.KC = D // 128
        self.AW, self.KW, self.GW, self.LW = AH * 128, KV * 128, GH * 128, LB * 128
        self.MIX = self.AW + self.GW + self.LW
        self.KCM = self.MIX // 128
        self.FC = DFF // 128
        self.o_aq = 0
        self.o_ak = self.AW
        self.o_av = self.o_ak + self.KW
        self.o_gq = self.o_av + self.KW
        self.o_gk = self.o_gq + self.GW
        self.o_gv = self.o_gk + self.GW
        self.o_gg = self.o_gv + self.GW
        self.o_gd = self.o_gg + self.GW
        self.o_lx = self.o_gd + 32
        self.o_ly = self.o_lx + self.LW
        self.INW = self.o_ly + self.LW
        self.NCT = NSEQ * SEQ
        self.NTOK = self.NCT + DSEQ
        self.f_aq = 0
        self.f_ak = self.f_aq + AH
        self.f_gq = self.f_ak + KV
        self.f_gk = self.f_gq + GH
        self.f_gg = self.f_gk + GH
        self.f_gd = self.f_gg + GH
        self.f_lx = self.f_gd + 1
        self.f_ly = self.f_lx + LB
        self.NF = self.f_ly + LB
        self.PTW = self.KW + self.GW
        self.vcol = {}
        n = 0

        def add(name, c):
            nonlocal n
            self.vcol[name] = n
            n += c
        add("cT", 2 * self.KC)
        add("fng", self.KC)
        for l in range(L):
            add("bada%d" % l, 6 * self.KC)
            add("n1g%d" % l, self.KC)
            add("n2g%d" % l, self.KC)
            add("sink%d" % l, AH)
            add("bdec%d" % l, 2 * GH)
            add("gnorm%d" % l, GH)
            add("convw%d" % l, 4 * LB)
            add("convb%d" % l, LB)
            add("br%d" % l, 2 * LB)
            add("bi%d" % l, 2 * LB)
            add("lam%d" % l, 2 * LB)
            add("lru0%d" % l, 2 * LB)
        self.NV = n

    def tiles(self):
        out = []
        for t0 in range(0, self.NCT, 512):
            out.append((t0, 0))
        for t0 in range(self.NCT, self.NTOK, 512):
            out.append((t0, 1))
        return out

    def tile_pairs(self, n=2):
        out = []
        for r in (0, 1):
            ts = [t0 for (t0, rr) in self.tiles() if rr == r]
            for i in range(0, len(ts), n):
                out.append((ts[i:i + n], r))
        return out

    def gla_seqs(self):
        return [(0, self.NCT, False, self.NSEQ), (self.NCT, self.DSEQ, True, 1)]

    def seqs(self):
        out = [(s * self.SEQ, self.SEQ, False, s) for s in range(self.NSEQ)]
        out.append((self.NCT, self.DSEQ, True, 0))
        return out


def feat_major(v):
    v = np.asarray(v, np.float32).reshape(-1, 128)
    return np.ascontiguousarray(v.T)


def mk(a, off, dims):
    return AP(a.tensor, a.offset + off, [list(a.ap[0])] + [list(d) for d in dims])


def bc_mid(a, reps):
    return AP(a.tensor, a.offset, [list(a.ap[0]), [0, reps], list(a.ap[1])])


def bc_last(a, n):
    return AP(a.tensor, a.offset, [list(a.ap[0]), list(a.ap[1]), [0, n]])


def rev(a):
    st, n = a.ap[1]
    return AP(a.tensor, a.offset + st * (n - 1), [list(a.ap[0]), [-st, n]])


class Slots:
    def __init__(self, K, name, shape, dt, n):
        self.t = [K.sb("%s%d" % (name, i), shape, dt) for i in range(n)]
        self.b = [Buf("%s%d" % (name, i)) for i in range(n)]
        self.i = 0

    def next(self):
        j = self.i % len(self.t)
        self.i += 1
        return self.t[j], self.b[j]


class K:
    pass


def build(cfg):
    c = cfg
    nc = bass.Bass("TRN2", target_bir_lowering=False)
    k = K()
    k.nc, k.c = nc, c
    L = c.L

    def din(name, shape, dt=F32):
        return nc.dram_tensor(name, list(shape), dt, kind="ExternalInput").ap()

    def dout(name, shape, dt=F32):
        return nc.dram_tensor(name, list(shape), dt, kind="ExternalOutput").ap()

    def dscr(name, shape, dt):
        return nc.dram_tensor(name, list(shape), dt).ap()

    k.xin = din("xin", [c.NTOK, c.D])
    k.ck = din("ck", [L, c.PAST, c.KW])
    k.cv = din("cv", [L, c.PAST, c.KW])
    k.sg = din("sg", [L, 2, c.GH, 128, 128])
    k.vecs = din("vecs", [128, c.NV])
    k.w_ada = din("w_ada", [L, c.D, 6 * c.D])
    k.w_in = din("w_in", [L, c.D, c.INW])
    k.w_dec = din("w_dec", [L, 2, 16, c.GW])
    k.w_r = din("w_r", [L, 2, c.LB, 128, 128])
    k.w_i = din("w_i", [L, 2, c.LB, 128, 128])
    k.w_out = din("w_out", [L, c.MIX, c.D])
    k.w_gu = din("w_gu", [L, c.D, 2 * c.DFF])
    k.w_down = din("w_down", [L, c.DFF, c.D])
    k.c_ident = din("c_ident", [128, 128])
    k.c_pm = din("c_pm", [128, 128])
    k.c_rope = din("c_rope", [128, 2, c.DSEQ])
    k.c_mlr = din("c_mlr", [128, 2, 512])
    k.c_gmask = din("c_gmask", [64, 2, 512])
    k.c_reset = din("c_reset", [128, 2048])

    k.y = dout("y", [c.NTOK, c.D])
    k.nk = dout("nk", [c.NSEQ, L, c.SEQ, c.KW])
    k.nv = dout("nv", [c.NSEQ, L, c.SEQ, c.KW])
    k.ngla = dout("ngla", [c.NSEQ, L, 2, c.GH, 128, 128])
    k.nlru = dout("nlru", [c.NSEQ, L, 2, c.LW])

    k.XT = dscr("XT", [c.KC, 128, c.NTOK], F32)
    k.HN = dscr("HN", [c.KC, 128, c.NTOK], BF16)
    k.PF = dscr("PF", [c.NF, 128, c.NTOK], BF16)
    k.PT = dscr("PT", [c.NTOK, c.PTW], BF16)
    k.MX = dscr("MX", [c.KCM, 128, c.NTOK], BF16)
    k.AC = dscr("AC", [c.FC, 128, c.NTOK], BF16)
    nt = c.NTOK // 512
    k.bXT = [Buf("XT%d" % i) for i in range(nt)]
    k.bHN = [Buf("HN%d" % i) for i in range(nt)]
    k.bPF = [Buf("PF%d" % i) for i in range(nt)]
    k.bPT = [Buf("PT%d" % i) for i in range(nt)]
    k.bMX = [Buf("MX%d" % i) for i in range(nt)]
    k.bAC = [Buf("AC%d" % i) for i in range(nt)]

    with contextlib.ExitStack() as gst:
        em = Em(nc, gst)
        k.em = em
        k.gst = gst
        k.st = gst
        k.uid = 0

        def _sb(name, shape, dt):
            k.uid += 1
            return k.st.enter_context(nc.sbuf_tensor("%s_%d" % (name, k.uid), list(shape), dt))
        k.sb = _sb
        k.pb = [gst.enter_context(nc.psum_tensor("pb%d" % i, [128, 512], F32)) for i in range(8)]
        k.bpb = [Buf("pb%d" % i, excl=True) for i in range(8)]
        k.pi = 0
        k.vec = k.sb("vec", [128, c.NV], F32)
        k.bvec = Buf("vec")
        em.dma("sp", k.vec[:], k.vecs, writes=[k.bvec])
        k.identf = k.sb("identf", [128, 128], F32)
        k.identb = k.sb("identb", [128, 128], BF16)
        k.onesb = k.sb("onesb", [128, 128], BF16)
        k.bconst = Buf("const")
        em.dma("sp", k.identf[:], k.c_ident, writes=[k.bconst])
        em.op("dve", lambda: nc.vector.tensor_copy(k.identb[:], k.identf[:]), reads=[k.bconst], writes=[k.bconst])
        em.op("dve", lambda: nc.vector.memset(k.onesb[:], 1.0), writes=[k.bconst])
        k.epst = k.sb("epst", [128, 1], F32)
        k.onet = k.sb("onet", [128, 1], F32)
        em.op("dve", lambda: nc.vector.memset(k.epst[:], EPS), writes=[k.bconst])
        em.op("dve", lambda: nc.vector.memset(k.onet[:], 1.0), writes=[k.bconst])
        k.mod = k.sb("mod", [128, L, 6 * c.KC, 2], F32)
        k.A = k.sb("modA", [128, L, 2, c.KC, 2], F32)
        k.bmod = Buf("mod")
        k.nbanks = 8
        k.sc = k.sb("ada_sc", [128, c.KC, 2], BF16)
        k.bsc = Buf("ada_sc")

        plist = [lambda: phase_transpose_in(k), lambda: phase_ada(k)]
        for l in range(L):
            plist += [lambda l=l: phase_norm(k, l, 0), lambda l=l: phase_inproj(k, l), lambda l=l: phase_attn(k, l),
                      lambda l=l: phase_gla(k, l), lambda l=l: phase_lru(k, l), lambda l=l: phase_outproj(k, l),
                      lambda l=l: phase_norm(k, l, 1), lambda l=l: phase_gu(k, l), lambda l=l: phase_down(k, l)]
        plist.append(lambda: phase_final(k))
        for f in plist[:NPHASES]:
            f()
        em.finish()
        print("EMSTAT n_ins", em.n_ins, {kk: vv for kk, vv in em.cnt.items()}, flush=True)
    return nc


def vcol(k, name, j=0, n=1):
    o = k.c.vcol[name] + j
    return k.vec[:, o:o + n]


def pbank(k):
    i = k.pi % k.nbanks
    k.pi += 1
    return k.pb[i], k.bpb[i]


@contextlib.contextmanager
def phase(k):
    with contextlib.ExitStack() as st:
        old = k.st
        k.st = st
        k.em.barrier()
        yield
        k.em.barrier()
        k.st = old


def phase_transpose_in(k):
    nc, em, c = k.nc, k.em, k.c
    with phase(k):
        xs = Slots(k, "ti_x", [128, c.D], F32, 2)
        xo = Slots(k, "ti_o", [128, c.KC, 512], F32, 2)
        ei = 0
        for (t0, r) in c.tiles():
            ot, bo = xo.next()
            for blk in range(4):
                tok = t0 + blk * 128
                xt, bx = xs.next()
                em.dma("sp", xt[:], k.xin[tok:tok + 128, :], writes=[bx])
                for c0 in range(0, c.KC, 4):
                    n = min(4, c.KC - c0)
                    pb, bp = pbank(k)
                    em.group("pe", [
                        (lambda j=j, pb=pb: nc.tensor.transpose(pb[:, j * 128:(j + 1) * 128],
                                                                xt[:, (c0 + j) * 128:(c0 + j + 1) * 128], k.identf[:]))
                        for j in range(n)], reads=[bx, k.bconst], writes=[bp])
                    evac(k, ei, ot[:, c0:c0 + n, blk * 128:(blk + 1) * 128],
                         pb[:, 0:n * 128].rearrange("p (c t) -> p c t", c=n), [bp], [bo])
                    ei += 1
            em.dma("pool", k.XT[:, :, t0:t0 + 512].rearrange("c p t -> p c t"), ot[:], reads=[bo], writes=[k.bXT[t0 // 512]])


def ada_epilogue(k, l, pb, bp):
    nc, em, c = k.nc, k.em, k.c
    KC = c.KC
    bada = vcol(k, "bada%d" % l, 0, 6 * KC)
    for r in range(2):
        em.op("dve", lambda r=r: nc.vector.tensor_tensor(
            k.mod[:, l, :, r], pb[:, 0:12 * KC].rearrange("p (m r) -> p m r", r=2)[:, :, r], bada, ALU.add),
            reads=[bp, k.bvec], writes=[k.bmod])
    for which in range(2):
        g = vcol(k, ("n1g%d" if which == 0 else "n2g%d") % l, 0, KC)
        base = KC + which * 3 * KC
        for r in range(2):
            em.op("dve", lambda which=which, r=r, g=g, base=base: nc.vector.scalar_tensor_tensor(
                k.A[:, l, which, :, r], k.mod[:, l, base:base + KC, r], 1.0, g, ALU.add, ALU.mult),
                reads=[k.bmod, k.bvec], writes=[k.bmod])


def phase_ada(k):
    nc, em, c = k.nc, k.em, k.c
    KC = c.KC
    with phase(k):
        sc, bsc = k.sc, k.bsc
        cT = vcol(k, "cT", 0, 2 * KC).rearrange("p (c r) -> p c r", r=2)
        em.op("act", lambda: nc.scalar.activation(sc[:], cT, AF.Silu), reads=[k.bvec], writes=[bsc])
        ws = Slots(k, "ada_w", [128, KC, 512], BF16, 3)
        NG = (6 * c.D) // 512
        for l in range(1):
            pb, bp = pbank(k)
            wv = k.w_ada[l].rearrange("(c p) n -> p c n", p=128)
            for g in range(NG):
                wt, bw = ws.next()
                em.dma("pool", wt[:], wv[:, :, g * 512:(g + 1) * 512], writes=[bw])
                for j in range(4):
                    m = g * 4 + j
                    em.group("pe", [
                        (lambda kc=kc, j=j, m=m, wt=wt, pb=pb: nc.tensor.matmul(
                            pb[:, 2 * m:2 * m + 2], lhsT=wt[:, kc, j * 128:(j + 1) * 128], rhs=sc[:, kc, :],
                            start=(kc == 0), stop=(kc == KC - 1)))
                        for kc in range(KC)], reads=[bw, bsc], writes=[bp])
            ada_epilogue(k, l, pb, bp)


def ada_bg(k, l, ws, pb, bp):
    nc, em, c = k.nc, k.em, k.c
    KC = c.KC
    n = 6 * KC
    wv = k.w_ada[l].rearrange("(c p) n -> p c n", p=128)
    tiles = {}

    def load(m):
        wt, bw = ws.next()
        em.dma("pool", wt[:], wv[:, :, m * 128:(m + 1) * 128], writes=[bw])
        tiles[m] = (wt, bw)
    for m in range(min(2, n)):
        load(m)
    for m in range(n):
        if m + 2 < n:
            load(m + 2)
        wt, bw = tiles.pop(m)
        em.group("pe", [
            (lambda kc=kc: nc.tensor.matmul(pb[:, 2 * m:2 * m + 2], lhsT=wt[:, kc, :], rhs=k.sc[:, kc, :],
                                            start=(kc == 0), stop=(kc == KC - 1)))
            for kc in range(KC)], reads=[bw, k.bsc], writes=[bp])
        yield
    ada_epilogue(k, l, pb, bp)


def norm_block(k, xt, bx, n, sq, bsq, rs, brs):
    nc, em, c = k.nc, k.em, k.c
    em.op("act", lambda: nc.scalar.activation(sq[:, :, 0:n], xt[:, :, 0:n], AF.Square), reads=[bx], writes=[bsq])
    pb, bp = pbank(k)
    em.group("pe", [
        (lambda kc=kc: nc.tensor.matmul(pb[:, 0:n], lhsT=k.onesb[:], rhs=sq[:, kc, 0:n],
                                        start=(kc == 0), stop=(kc == c.KC - 1)))
        for kc in range(c.KC)], reads=[bsq, k.bconst], writes=[bp])
    em.op("act", lambda: nc.scalar.activation(rs[:, 0:n], pb[:, 0:n], AF.Ln, bias=k.epst[:], scale=1.0 / c.D),
          reads=[bp, k.bconst], writes=[brs])
    em.op("act", lambda: nc.scalar.activation(rs[:, 0:n], rs[:, 0:n], AF.Exp, scale=-0.5), reads=[brs], writes=[brs])


def phase_norm(k, l, which):
    nc, em, c = k.nc, k.em, k.c
    KC = c.KC
    with phase(k):
        xs = Slots(k, "nm_x", [128, KC, 512], F32, 2)
        hs = Slots(k, "nm_h", [128, KC, 512], BF16, 2)
        rss = Slots(k, "nm_r", [128, 512], F32, 2)
        for (t0, r) in c.tiles():
            ti = t0 // 512
            xt, bx = xs.next()
            em.dma("sp", xt[:], k.XT[:, :, t0:t0 + 512].rearrange("c p t -> p c t"), reads=[k.bXT[ti]], writes=[bx])
            ht, bh = hs.next()
            rs, brs = rss.next()
            norm_block(k, xt, bx, 512, ht, bh, rs, brs)
            em.op("dve", lambda: nc.vector.tensor_tensor(xt[:], xt[:], bc_mid(rs[:], KC), ALU.mult),
                  reads=[bx, brs], writes=[bx])
            for kc in range(KC):
                em.op("act", lambda kc=kc: nc.scalar.activation(
                    ht[:, kc, :], xt[:, kc, :], AF.Identity,
                    bias=k.mod[:, l, which * 3 * KC + kc, r:r + 1], scale=k.A[:, l, which, kc, r:r + 1]),
                    reads=[bx, k.bmod], writes=[bh])
            em.dma("pool", k.HN[:, :, t0:t0 + 512].rearrange("c p t -> p c t"), ht[:], reads=[bh], writes=[k.bHN[ti]])


def load_act_tile(k, slots, src, bsrc, KCin, t0):
    at, ba = slots.next()
    k.em.dma("sp", at[:, 0:KCin, :], src[:, :, t0:t0 + 512].rearrange("c p t -> p c t"),
             reads=[bsrc[t0 // 512]], writes=[ba])
    return at, ba


_XB = {}


def load_w(k, slots, W, KCin, c0, n):
    wt, bw = slots.next()
    src = W[:, c0:c0 + n].rearrange("(c p) n -> p c n", p=128)
    if KCin <= 32:
        k.em.dma("pool", wt[:, 0:KCin, 0:n], src, writes=[bw])
        return wt, bw
    bw2 = _XB.setdefault(id(bw), Buf("xb"))
    h = KCin // 2
    k.em.dma("pool", wt[:, 0:h, 0:n], src[:, 0:h, :], writes=[bw])
    k.em.dma("pool", wt[:, h:KCin, 0:n], src[:, h:KCin, :], writes=[bw2])
    return wt, (bw, bw2)


def mm_fm(k, wt, bw, at, ba, KCin, j, M=128):
    nc = k.nc
    pb, bp = pbank(k)
    k.em.group("pe", [
        (lambda kc=kc: nc.tensor.matmul(pb[0:M, :], lhsT=wt[:, kc, j * 128:j * 128 + M], rhs=at[:, kc, :],
                                        start=(kc == 0), stop=(kc == KCin - 1)))
        for kc in range(KCin)], reads=(list(bw) if isinstance(bw, tuple) else [bw]) + [ba], writes=[bp])
    return pb, bp


def mm_tm(k, wt, bw, at, ba, KCin, tb, n):
    nc = k.nc
    pb, bp = pbank(k)
    k.em.group("pe", [
        (lambda kc=kc: nc.tensor.matmul(pb[:, 0:n], lhsT=at[:, kc, tb * 128:(tb + 1) * 128], rhs=wt[:, kc, 0:n],
                                        start=(kc == 0), stop=(kc == KCin - 1)))
        for kc in range(KCin)], reads=[bw, ba], writes=[bp])
    return pb, bp


def evac(k, i, dst, src, reads, writes):
    nc = k.nc
    if i % 2 == 0:
        k.em.op("act", lambda: nc.scalar.copy(dst, src), reads=reads, writes=writes)
    else:
        k.em.op("dve", lambda: nc.vector.tensor_copy(dst, src), reads=reads, writes=writes)


def phase_inproj(k, l):
    nc, em, c = k.nc, k.em, k.c
    KC = c.KC
    W = k.w_in[l]
    fm = []

    def add_fm(o, width, f):
        for g0 in range(0, width, 512):
            n = min(512, width - g0)
            fm.append((o + g0, n, f + g0 // 128))
    add_fm(c.o_aq, c.AW, c.f_aq)
    add_fm(c.o_ak, c.KW, c.f_ak)
    add_fm(c.o_gq, c.GW, c.f_gq)
    add_fm(c.o_gk, c.GW, c.f_gk)
    add_fm(c.o_gg, c.GW, c.f_gg)
    add_fm(c.o_lx, c.LW, c.f_lx)
    add_fm(c.o_ly, c.LW, c.f_ly)
    tm = []
    for g0 in range(0, c.KW, 512):
        n = min(512, c.KW - g0)
        tm.append((c.o_av + g0, n, g0, "nv"))
    for g0 in range(0, c.GW, 512):
        n = min(512, c.GW - g0)
        tm.append((c.o_gv + g0, n, c.KW + g0, None))
    tmk = []
    for g0 in range(0, c.KW, 512):
        n = min(512, c.KW - g0)
        tmk.append((c.o_ak + g0, n, g0, "nk"))
    with phase(k):
        acts = Slots(k, "ip_a", [128, KC, 512], BF16, 2)
        ws = Slots(k, "ip_w", [128, KC, 512], BF16, 3)
        stg = Slots(k, "ip_s", [128, 4, 512], BF16, 3)
        stt = Slots(k, "ip_t", [128, 512], BF16, 3)
        stf = Slots(k, "ip_f", [128, 512], F32, 3)
        ei = 0
        for (t0s, r) in c.tile_pairs():
            ats = [load_act_tile(k, acts, k.HN, k.bHN, KC, t0) for t0 in t0s]
            for (o, n, f) in fm:
                wt, bw = load_w(k, ws, W, KC, o, n)
                for t0, (at, ba) in zip(t0s, ats):
                    ti = t0 // 512
                    sg_, bs = stg.next()
                    nj = n // 128
                    for j in range(nj):
                        pb, bp = mm_fm(k, wt, bw, at, ba, KC, j)
                        evac(k, ei, sg_[:, j, :], pb[:], [bp], [bs])
                        ei += 1
                    em.dma("sp", k.PF[f:f + nj, :, t0:t0 + 512].rearrange("c p t -> p c t"), sg_[:, 0:nj, :],
                           reads=[bs], writes=[k.bPF[ti]])
            wt, bw = load_w(k, ws, W, KC, c.o_gd, 32)
            for t0, (at, ba) in zip(t0s, ats):
                ti = t0 // 512
                sg_, bs = stg.next()
                pb, bp = mm_fm(k, wt, bw, at, ba, KC, 0, M=32)
                evac(k, ei, sg_[0:32, 0, :], pb[0:32, :], [bp], [bs])
                ei += 1
                em.dma("sp", k.PF[c.f_gd, 0:32, t0:t0 + 512], sg_[0:32, 0, :], reads=[bs], writes=[k.bPF[ti]])
            groups = list(tm) + (list(tmk) if r == 0 else [])
            for (o, n, pc, oname) in groups:
                wt, bw = load_w(k, ws, W, KC, o, n)
                for (t0, tb, at, ba) in [(t0, tb, at, ba) for t0, (at, ba) in zip(t0s, ats) for tb in range(4)]:
                    ti = t0 // 512
                    pb, bp = mm_tm(k, wt, bw, at, ba, KC, tb, n)
                    tok = t0 + tb * 128
                    if oname != "nk":
                        st_, bt_ = stt.next()
                        evac(k, ei, st_[:, 0:n], pb[:, 0:n], [bp], [bt_])
                        ei += 1
                        em.dma("sp", k.PT[tok:tok + 128, pc:pc + n], st_[:, 0:n], reads=[bt_], writes=[k.bPT[ti]])
                    if oname is not None and r == 0:
                        sf_, bf_ = stf.next()
                        evac(k, ei, sf_[:, 0:n], pb[:, 0:n], [bp], [bf_])
                        ei += 1
                        s_i, s_o = tok // c.SEQ, tok % c.SEQ
                        dst = (k.nv if oname == "nv" else k.nk)[s_i, l, s_o:s_o + 128, pc:pc + n]
                        em.dma("sp", dst, sf_[:, 0:n], reads=[bf_])


def phase_attn(k, l):
    nc, em, c = k.nc, k.em, k.c
    scale = 128.0 ** -0.5
    NQ = c.DSEQ
    with phase(k):
        esink = k.sb("at_es", [128, c.AH], F32)
        bes = Buf("es")
        em.op("act", lambda: nc.scalar.activation(esink[:], vcol(k, "sink%d" % l, 0, c.AH), AF.Exp),
              reads=[k.bvec], writes=[bes])
        mlr_f = k.sb("at_mf", [128, 2, 512], F32)
        mlr = k.sb("at_mb", [128, 2, 512], BF16)
        bm = Buf("mlr")
        em.dma("sp", mlr_f[:], k.c_mlr, writes=[bm])
        em.op("dve", lambda: nc.vector.tensor_copy(mlr[:], mlr_f[:]), reads=[bm], writes=[bm])
        pmf = k.sb("at_pf", [128, 128], F32)
        pmb = k.sb("at_pb", [128, 128], BF16)
        em.dma("sp", pmf[:], k.c_pm, writes=[bm])
        em.op("dve", lambda: nc.vector.tensor_copy(pmb[:], pmf[:]), reads=[bm], writes=[bm])
        rope = k.sb("at_rp", [128, 2, NQ], F32)
        em.dma("sp", rope[:], k.c_rope, writes=[bm])
        qs = Slots(k, "at_q", [128, 4, NQ], BF16, 2)
        ks = Slots(k, "at_k", [128, NQ], BF16, 2)
        vs = Slots(k, "at_v", [128, NQ // 128, 128], BF16, 2)
        qr = Slots(k, "at_qr", [128, 4, NQ], BF16, 1)
        kr = Slots(k, "at_kr", [128, NQ], BF16, 1)
        outs = Slots(k, "at_o", [128, 4, NQ], BF16, 2)
        pts = Slots(k, "at_p", [128, 5, 512], BF16, 2)
        dens = Slots(k, "at_d", [128, 512], F32, 2)
        t1s = Slots(k, "at_t1", [128, 512], F32, 2)
        t2s = Slots(k, "at_t2", [128, 512], F32, 2)
        kcf = Slots(k, "at_kcf", [128, 2, 128], F32, 1)
        kct = Slots(k, "at_kct", [128, c.PAST], BF16, 1)
        vcs = Slots(k, "at_vc", [128, c.PAST // 128, 128], BF16, 1)

        def rope_apply(dst, src, bsrc, bdst, n0, n):
            pb, bp = pbank(k)
            em.op("pe", lambda: nc.tensor.matmul(pb[:, 0:n], lhsT=pmb[:], rhs=src, start=True, stop=True),
                  reads=[bsrc, bm], writes=[bp])
            t1, b1 = t1s.next()
            t2, b2 = t2s.next()
            em.op("pool", lambda: nc.gpsimd.tensor_tensor(t1[:, 0:n], src, rope[:, 0, n0:n0 + n], ALU.mult),
                  reads=[bsrc, bm], writes=[b1])
            em.op("dve", lambda: nc.vector.tensor_tensor(t2[:, 0:n], pb[:, 0:n], rope[:, 1, n0:n0 + n], ALU.mult),
                  reads=[bp, bm], writes=[b2])
            em.op("dve", lambda: nc.vector.tensor_tensor(dst, t1[:, 0:n], t2[:, 0:n], ALU.add),
                  reads=[b1, b2], writes=[bdst])

        def attend(Q, bq, qb, chunks, OUT, bo, kv):
            n = len(chunks)
            pt, bpt = pts.next()
            rhs = Q[:, :, qb * 128:(qb + 1) * 128]
            for i, (kT, V, mi, bk, bv) in enumerate(chunks):
                pb, bp = pbank(k)
                out = pb[:].rearrange("p (g q) -> p g q", g=4)
                fns = [lambda out=out, kT=kT, mi=mi: nc.tensor.matmul(out, lhsT=kT, rhs=rhs, start=True, stop=(mi is None))]
                if mi is not None:
                    fns.append(lambda pb=pb, mi=mi: nc.tensor.matmul(pb[:], lhsT=k.identb[:], rhs=mlr[:, mi, :],
                                                                    start=False, stop=True))
                em.group("pe", fns, reads=[bq, bk, bm, k.bconst], writes=[bp])
                em.op("act", lambda i=i, pb=pb: nc.scalar.activation(pt[:, i, :], pb[:], AF.Exp, scale=scale),
                      reads=[bp], writes=[bpt])
            po, bpo = pbank(k)
            em.group("pe", [
                (lambda i=i, V=chunks[i][1]: nc.tensor.matmul(po[:], lhsT=V, rhs=pt[:, i, :],
                                                              start=(i == 0), stop=(i == n - 1)))
                for i in range(n)], reads=[bpt] + [ch[4] for ch in chunks], writes=[bpo])
            pd, bpd = pbank(k)
            em.group("pe", [
                (lambda i=i: nc.tensor.matmul(pd[:], lhsT=k.onesb[:], rhs=pt[:, i, :],
                                              start=(i == 0), stop=(i == n - 1)))
                for i in range(n)], reads=[bpt, k.bconst], writes=[bpd])
            dn, bd = dens.next()
            for g in range(4):
                h = kv * 4 + g
                em.op("dve", lambda g=g, h=h: nc.vector.tensor_scalar(
                    dn[:, g * 128:(g + 1) * 128], pd[:, g * 128:(g + 1) * 128], esink[:, h:h + 1], None, ALU.add),
                    reads=[bpd, bes], writes=[bd])
            em.op("dve", lambda: nc.vector.reciprocal(dn[:], dn[:]), reads=[bd], writes=[bd])
            em.op("dve", lambda: nc.vector.tensor_tensor(
                OUT[:, :, qb * 128:(qb + 1) * 128], po[:].rearrange("p (g q) -> p g q", g=4),
                dn[:].rearrange("p (g q) -> p g q", g=4), ALU.mult), reads=[bpo, bd], writes=[bo])

        for (t0, N, lat, si) in c.seqs():
            tis = list(range(t0 // 512, (t0 + N + 511) // 512))
            rPF = [k.bPF[i] for i in tis]
            rPT = [k.bPT[i] for i in tis]
            for kv in range(c.KV):
                Q, bq = qs.next()
                kT, bk = ks.next()
                V, bv = vs.next()
                em.dma("sp", Q[:, :, 0:N], k.PF[c.f_aq + kv * 4:c.f_aq + kv * 4 + 4, :, t0:t0 + N]
                       .rearrange("c p t -> p c t"), reads=rPF, writes=[bq])
                em.dma("sp", kT[:, 0:N], k.PF[c.f_ak + kv, :, t0:t0 + N], reads=rPF, writes=[bk])
                em.dma("sp", V[:, 0:N // 128, :], k.PT[t0:t0 + N, kv * 128:(kv + 1) * 128]
                       .rearrange("(b p) d -> p b d", p=128), reads=rPT, writes=[bv])
                OUT, bo = outs.next()
                if not lat:
                    for qb in range(N // 128):
                        chunks = [(kT[:, i * 128:(i + 1) * 128], V[:, i, :], None, bk, bv) for i in range(N // 128)]
                        attend(Q, bq, qb, chunks, OUT, bo, kv)
                else:
                    Qr, bqr = qr.next()
                    Kr, bkr = kr.next()
                    for n0 in range(0, N, 512):
                        for g in range(4):
                            rope_apply(Qr[:, g, n0:n0 + 512], Q[:, g, n0:n0 + 512], bq, bqr, n0, 512)
                        rope_apply(Kr[:, n0:n0 + 512], kT[:, n0:n0 + 512], bk, bkr, n0, 512)
                    kc_f, bkc = kcf.next()
                    kc_t, bkt = kct.next()
                    vc, bvc = vcs.next()
                    em.dma("sp", kc_f[:], k.ck[l, :, kv * 128:(kv + 1) * 128].rearrange("(b p) d -> p b d", p=128),
                           writes=[bkc])
                    em.dma("pool", vc[:], k.cv[l, :, kv * 128:(kv + 1) * 128].rearrange("(b p) d -> p b d", p=128),
                           writes=[bvc])
                    for b in range(c.PAST // 128):
                        pb, bp = pbank(k)
                        em.op("pe", lambda b=b, pb=pb: nc.tensor.transpose(pb[:, 0:128], kc_f[:, b, :], k.identf[:]),
                              reads=[bkc, k.bconst], writes=[bp])
                        em.op("act", lambda b=b, pb=pb: nc.scalar.copy(kc_t[:, b * 128:(b + 1) * 128], pb[:, 0:128]),
                              reads=[bp], writes=[bkt])
                    nqb = N // 128
                    for qb in range(nqb):
                        chunks = []
                        if qb > 0:
                            chunks.append((Kr[:, (qb - 1) * 128:qb * 128], V[:, qb - 1, :], 0, bkr, bv))
                        chunks.append((Kr[:, qb * 128:(qb + 1) * 128], V[:, qb, :], None, bkr, bv))
                        if qb < nqb - 1:
                            chunks.append((Kr[:, (qb + 1) * 128:(qb + 2) * 128], V[:, qb + 1, :], 1, bkr, bv))
                        for b in range(c.PAST // 128):
                            chunks.append((kc_t[:, b * 128:(b + 1) * 128], vc[:, b, :], None, bkt, bvc))
                        attend(Qr, bqr, qb, chunks, OUT, bo, kv)
                em.dma("pool", k.MX[kv * 4:kv * 4 + 4, :, t0:t0 + N].rearrange("c p t -> p c t"), OUT[:, :, 0:N],
                       reads=[bo], writes=[k.bMX[i] for i in tis])


def phase_gla(k, l):
    nc, em, c = k.nc, k.em, k.c
    NM = max(c.SEQ, c.DSEQ)
    NCM = NM // 64
    with phase(k):
        wdec = k.sb("gl_wd", [16, 2, c.GW], BF16)
        bwd = Buf("wdec")
        em.dma("pool", wdec[:], k.w_dec[l].rearrange("d r n -> r d n"), writes=[bwd])
        negb = k.sb("gl_nb", [128, 2 * c.GH], F32)
        em.op("dve", lambda: nc.vector.tensor_scalar(negb[:], vcol(k, "bdec%d" % l, 0, 2 * c.GH), -1.0, None, ALU.mult),
              reads=[k.bvec], writes=[bwd])
        gmask = k.sb("gl_gm", [64, 2, 512], BF16)
        em.dma("pool", gmask[:], k.c_gmask, writes=[bwd])
        resetm = k.sb("gl_rm", [128, NM], BF16)
        em.dma("pool", resetm[:], k.c_reset[:, 0:NM], writes=[bwd])

        gd = [Slots(k, "gl_gd%d" % d, [16, NM], BF16, 1) for d in range(2)]
        qs = Slots(k, "gl_q", [128, NM], BF16, 2)
        ks_ = Slots(k, "gl_k", [128, NM], BF16, 2)
        gs = Slots(k, "gl_g", [128, NM], BF16, 1)
        vs = Slots(k, "gl_v", [64, NCM, 128], BF16, 1)
        e1 = Slots(k, "gl_e1", [128, NM], F32, 1)
        l1 = e1
        cum = Slots(k, "gl_cum", [128, NM], F32, 1)
        ecs = Slots(k, "gl_ec", [128, NM], F32, 1)
        dls = Slots(k, "gl_dl", [128, NM], F32, 1)
        ex = Slots(k, "gl_ex", [128, NM], F32, 1)
        Et = [Slots(k, "gl_E%d" % d, [128, NCM], F32, 1) for d in range(2)]
        qin = [Slots(k, "gl_qi%d" % d, [128, NM], BF16, 1) for d in range(2)]
        kin = [Slots(k, "gl_ki%d" % d, [128, NM], BF16, 1) for d in range(2)]
        kdT = Slots(k, "gl_kdT", [128, NM], BF16, 1)
        kd64 = [Slots(k, "gl_kd%d" % d, [64, NCM, 128], BF16, 1) for d in range(2)]
        STf = [Slots(k, "gl_sf%d" % d, [128, NCM + 1, 128], F32, 1) for d in range(2)]
        STb = [Slots(k, "gl_sb%d" % d, [128, NCM, 128], BF16, 1) for d in range(2)]
        atts = [Slots(k, "gl_at%d" % d, [64, 512], BF16, 2) for d in range(2)]
        osq = Slots(k, "gl_osq", [128, 512], BF16, 2)
        rss = Slots(k, "gl_rs", [128, 512], F32, 2)
        sgs = Slots(k, "gl_sg", [128, 512], F32, 2)
        tts = Slots(k, "gl_tt", [128, 512], F32, 2)
        obs = Slots(k, "gl_ob", [128, 512], BF16, 2)
        fins = [Slots(k, "gl_fin%d" % d, [128, c.NSEQ, 128], F32, 1) for d in range(2)]

        CPS = c.SEQ // 64
        for (t0, N, lat, nsub) in c.gla_seqs():
            NCk = N // 64
            G = min(N, 512)
            CG = G // 64
            tis = list(range(t0 // 512, (t0 + N + 511) // 512))
            rPF = [k.bPF[i] for i in tis]
            rPT = [k.bPT[i] for i in tis]
            gdt = []
            for d in range(2):
                g_, bg_ = gd[d].next()
                em.dma("sp", g_[:, 0:N], k.PF[c.f_gd, d * 16:(d + 1) * 16, t0:t0 + N], reads=rPF, writes=[bg_])
                gdt.append((g_, bg_))
            for h in range(c.GH):
                q_, bq = qs.next()
                k_, bk = ks_.next()
                g_, bg = gs.next()
                v_, bv = vs.next()
                em.dma("sp", q_[:, 0:N], k.PF[c.f_gq + h, :, t0:t0 + N], reads=rPF, writes=[bq])
                em.dma("sp", k_[:, 0:N], k.PF[c.f_gk + h, :, t0:t0 + N], reads=rPF, writes=[bk])
                em.dma("sp", g_[:, 0:N], k.PF[c.f_gg + h, :, t0:t0 + N], reads=rPF, writes=[bg])
                em.dma("sp", v_[:, 0:NCk, :], k.PT[t0:t0 + N, c.KW + h * 128:c.KW + (h + 1) * 128]
                       .rearrange("(c s) v -> s c v", s=64), reads=rPT, writes=[bv])
                dd = []
                for d in range(2):
                    gdd, bgd = gdt[d]
                    e1t, be1 = e1.next()
                    for n0 in range(0, N, G):
                        pb, bp = pbank(k)
                        em.op("pe", lambda pb=pb, n0=n0, d=d, gdd=gdd: nc.tensor.matmul(
                            pb[:, 0:G], lhsT=wdec[:, d, h * 128:(h + 1) * 128], rhs=gdd[:, n0:n0 + G],
                            start=True, stop=True), reads=[bgd, bwd], writes=[bp])
                        em.op("act", lambda pb=pb, n0=n0, d=d: nc.scalar.activation(
                            e1t[:, n0:n0 + G], pb[:, 0:G], AF.Exp, bias=negb[:, d * c.GH + h:d * c.GH + h + 1], scale=-1.0),
                            reads=[bp, bwd], writes=[be1])
                    l1t, bl1 = l1.next()
                    em.op("act", lambda: nc.scalar.activation(l1t[:, 0:N], e1t[:, 0:N], AF.Ln, bias=k.onet[:], scale=1.0),
                          reads=[be1, k.bconst], writes=[bl1])
                    cm, bcm = cum.next()
                    em.op("dve", lambda: nc.vector.tensor_tensor_scan(cm[:, 0:N], resetm[:, 0:N], l1t[:, 0:N], 0.0,
                                                                      ALU.mult, ALU.add), reads=[bl1, bwd], writes=[bcm])
                    tot = mk(cm[:], 63, [[64, NCk]])
                    totb = bc_last(tot, 64)
                    cm3 = cm[:, 0:N].rearrange("p (c s) -> p c s", s=64)
                    l13 = l1t[:, 0:N].rearrange("p (c s) -> p c s", s=64)
                    dl, bdl = dls.next()
                    dl3 = dl[:, 0:N].rearrange("p (c s) -> p c s", s=64)
                    if d == 0:
                        ec, bec = cm, bcm
                        em.op("dve", lambda: nc.vector.tensor_tensor(dl3, cm3, totb, ALU.subtract),
                              reads=[bcm], writes=[bdl])
                    else:
                        ec, bec = ecs.next()
                        ec3 = ec[:, 0:N].rearrange("p (c s) -> p c s", s=64)
                        em.op("dve", lambda: nc.vector.tensor_tensor(ec3, totb, cm3, ALU.subtract),
                              reads=[bcm], writes=[bec])
                        em.op("pool", lambda: nc.gpsimd.tensor_tensor(ec[:, 0:N], ec[:, 0:N], l1t[:, 0:N], ALU.add),
                              reads=[bec, bl1], writes=[bec])
                        em.op("pool", lambda: nc.gpsimd.tensor_tensor(dl[:, 0:N], l1t[:, 0:N], cm[:, 0:N], ALU.subtract),
                              reads=[bl1, bcm], writes=[bdl])
                    E, bE = Et[d].next()
                    em.op("act", lambda: nc.scalar.activation(E[:, 0:NCk], tot, AF.Exp, scale=-1.0 / 16),
                          reads=[bcm], writes=[bE])
                    qi, bqi = qin[d].next()
                    ki, bki = kin[d].next()
                    kd, bkd = kdT.next()
                    x1, bx1 = ex.next()
                    em.op("act", lambda: nc.scalar.activation(x1[:, 0:N], ec[:, 0:N], AF.Exp, scale=-1.0 / 16),
                          reads=[bec], writes=[bx1])
                    em.op("dve", lambda: nc.vector.scalar_tensor_tensor(qi[:, 0:N], q_[:, 0:N], 128.0 ** -0.5, x1[:, 0:N],
                                                                        ALU.mult, ALU.mult), reads=[bq, bx1], writes=[bqi])
                    x2, bx2 = ex.next()
                    em.op("act", lambda: nc.scalar.activation(x2[:, 0:N], ec[:, 0:N], AF.Exp, scale=1.0 / 16),
                          reads=[bec], writes=[bx2])
                    em.op("pool", lambda: nc.gpsimd.tensor_tensor(ki[:, 0:N], k_[:, 0:N], x2[:, 0:N], ALU.mult),
                          reads=[bk, bx2], writes=[bki])
                    x3, bx3 = ex.next()
                    em.op("act", lambda: nc.scalar.activation(x3[:, 0:N], dl[:, 0:N], AF.Exp, scale=1.0 / 16),
                          reads=[bdl], writes=[bx3])
                    em.op("dve", lambda: nc.vector.tensor_tensor(kd[:, 0:N], k_[:, 0:N], x3[:, 0:N], ALU.mult),
                          reads=[bk, bx3], writes=[bkd])
                    k64, bk64 = kd64[d].next()
                    for n0 in range(0, N, G):
                        pb, bp = pbank(k)
                        pbb = pb[:].bitcast(BF16)
                        em.group("pe", [
                            (lambda ci=ci, pbb=pbb, n0=n0: nc.tensor.transpose(
                                pbb[0:64, ci * 128:(ci + 1) * 128], kd[:, n0 + ci * 64:n0 + (ci + 1) * 64], k.identb[:]))
                            for ci in range(CG)], reads=[bkd, k.bconst], writes=[bp])
                        c0 = n0 // 64
                        em.op("act", lambda pbb=pbb, c0=c0: nc.scalar.copy(
                            k64[:, c0:c0 + CG, :], pbb[0:64, 0:CG * 128].rearrange("p (c d) -> p c d", d=128)),
                            reads=[bp], writes=[bk64])
                    sf, bsf = STf[d].next()
                    if lat:
                        em.dma("sp", sf[:, 0, :], k.sg[l, d, h], writes=[bsf])
                    else:
                        em.op("pool", lambda: nc.gpsimd.memset(sf[:, 0:NCk + 1, :], 0.0), writes=[bsf])
                        fin, bfin = fins[d].next()
                    order = list(range(NCk)) if d == 0 else list(range(NCk - 1, -1, -1))
                    for g0 in range(0, NCk, 4):
                        pb, bp = pbank(k)
                        em.group("pe", [
                            (lambda j=j, cc=order[g0 + j], pb=pb: nc.tensor.matmul(
                                pb[:, j * 128:(j + 1) * 128], lhsT=k64[:, cc, :], rhs=v_[:, cc, :], start=True, stop=True))
                            for j in range(min(4, NCk - g0))], reads=[bk64, bv], writes=[bp])
                        for j in range(min(4, NCk - g0)):
                            idx = g0 + j
                            cc = order[idx]
                            last = (cc % CPS == CPS - 1) if d == 0 else (cc % CPS == 0)
                            if (not lat) and last:
                                dst, wb = fin[:, cc // CPS, :], bfin
                            else:
                                dst, wb = sf[:, idx + 1, :], bsf
                            em.op("dve", lambda idx=idx, cc=cc, j=j, pb=pb, dst=dst: nc.vector.scalar_tensor_tensor(
                                dst, sf[:, idx, :], E[:, cc:cc + 1], pb[:, j * 128:(j + 1) * 128],
                                ALU.mult, ALU.add), reads=[bsf, bE, bp], writes=[wb])
                    sb_, bsb = STb[d].next()
                    em.op("act", lambda: nc.scalar.copy(sb_[:, 0:NCk, :], sf[:, 0:NCk, :]), reads=[bsf], writes=[bsb])
                    if not lat:
                        em.dma("pool", k.ngla[:, l, d, h].rearrange("s p v -> p s v"), fin[:], reads=[bfin])
                    dd.append((qi, bqi, ki, bki, sb_, bsb))
                for n0 in range(0, N, G):
                    c0 = n0 // 64
                    att = []
                    for d in range(2):
                        qi, bqi, ki, bki, sb_, bsb = dd[d]
                        pa, bpa = pbank(k)
                        em.group("pe", [
                            (lambda ci=ci, pa=pa, qi=qi, ki=ki: nc.tensor.matmul(
                                pa[0:64, ci * 64:(ci + 1) * 64], lhsT=ki[:, n0 + ci * 64:n0 + (ci + 1) * 64],
                                rhs=qi[:, n0 + ci * 64:n0 + (ci + 1) * 64], start=True, stop=True))
                            for ci in range(CG)], reads=[bqi, bki], writes=[bpa])
                        at_, bat = atts[d].next()
                        em.op("dve", lambda pa=pa, at_=at_, d=d: nc.vector.tensor_tensor(
                            at_[:, 0:G], pa[0:64, 0:G], gmask[:, d, 0:G], ALU.mult), reads=[bpa, bwd], writes=[bat])
                        att.append((at_, bat))
                    po, bpo = pbank(k)
                    fns = []
                    for ci in range(CG):
                        cc = c0 + ci
                        sl = slice(ci * 64, (ci + 1) * 64)
                        tsl = slice(n0 + ci * 64, n0 + (ci + 1) * 64)
                        fns.append(lambda sl=sl, cc=cc: nc.tensor.matmul(po[:, sl], lhsT=v_[:, cc, :], rhs=att[0][0][:, sl],
                                                                        start=True, stop=False))
                        fns.append(lambda sl=sl, cc=cc, tsl=tsl: nc.tensor.matmul(
                            po[:, sl], lhsT=dd[0][4][:, cc, :], rhs=dd[0][0][:, tsl], start=False, stop=False))
                        fns.append(lambda sl=sl, cc=cc: nc.tensor.matmul(po[:, sl], lhsT=v_[:, cc, :], rhs=att[1][0][:, sl],
                                                                        start=False, stop=False))
                        fns.append(lambda sl=sl, cc=cc, tsl=tsl: nc.tensor.matmul(
                            po[:, sl], lhsT=dd[1][4][:, NCk - 1 - cc, :], rhs=dd[1][0][:, tsl], start=False, stop=True))
                    em.group("pe", fns, reads=[bv, att[0][1], att[1][1], dd[0][1], dd[1][1], dd[0][5], dd[1][5]],
                             writes=[bpo])
                    oq, boq = osq.next()
                    em.op("act", lambda: nc.scalar.activation(oq[:, 0:G], po[:, 0:G], AF.Square), reads=[bpo], writes=[boq])
                    pn, bpn = pbank(k)
                    em.op("pe", lambda: nc.tensor.matmul(pn[:, 0:G], lhsT=k.onesb[:], rhs=oq[:, 0:G], start=True, stop=True),
                          reads=[boq, k.bconst], writes=[bpn])
                    rs, brs = rss.next()
                    em.op("act", lambda: nc.scalar.activation(rs[:, 0:G], pn[:, 0:G], AF.Ln, bias=k.epst[:], scale=1.0 / 128),
                          reads=[bpn, k.bconst], writes=[brs])
                    em.op("act", lambda: nc.scalar.activation(rs[:, 0:G], rs[:, 0:G], AF.Exp, scale=-0.5),
                          reads=[brs], writes=[brs])
                    sg_, bsg = sgs.next()
                    em.op("act", lambda: nc.scalar.activation(sg_[:, 0:G], g_[:, n0:n0 + G], AF.Silu), reads=[bg], writes=[bsg])
                    tt, btt = tts.next()
                    em.op("dve", lambda: nc.vector.scalar_tensor_tensor(
                        tt[:, 0:G], po[:, 0:G], vcol(k, "gnorm%d" % l, h, 1), rs[:, 0:G], ALU.mult, ALU.mult),
                        reads=[bpo, brs, k.bvec], writes=[btt])
                    ob, bob = obs.next()
                    em.op("pool", lambda: nc.gpsimd.tensor_tensor(ob[:, 0:G], tt[:, 0:G], sg_[:, 0:G], ALU.mult),
                          reads=[btt, bsg], writes=[bob])
                    em.dma("pool", k.MX[c.AH + h, :, t0 + n0:t0 + n0 + G], ob[:, 0:G], reads=[bob],
                           writes=[k.bMX[(t0 + n0) // 512]])


def phase_lru(k, l):
    nc, em, c = k.nc, k.em, k.c
    NM = max(c.SEQ, c.DSEQ)
    LB = c.LB
    with phase(k):
        wr = k.sb("lr_wr", [128, 2 * LB, 128], BF16)
        wi = k.sb("lr_wi", [128, 2 * LB, 128], BF16)
        bw = Buf("lruw")
        em.dma("pool", wr[:], k.w_r[l].rearrange("d b i j -> i (d b) j"), writes=[bw])
        em.dma("pool", wi[:], k.w_i[l].rearrange("d b i j -> i (d b) j"), writes=[bw])
        cl = k.sb("lr_cl", [128, 2 * LB], F32)
        em.op("act", lambda: nc.scalar.activation(cl[:], vcol(k, "lam%d" % l, 0, 2 * LB), AF.Exp, scale=-1.0),
              reads=[k.bvec], writes=[bw])
        em.op("act", lambda: nc.scalar.activation(cl[:], cl[:], AF.Ln, bias=k.onet[:], scale=1.0),
              reads=[bw, k.bconst], writes=[bw])
        em.op("dve", lambda: nc.vector.tensor_scalar(cl[:], cl[:], -8.0, None, ALU.mult), reads=[bw], writes=[bw])
        fs = k.sb("lr_fs", [128, c.NSEQ * 2 * LB], F32)
        bfs = Buf("fs")
        xs = Slots(k, "lr_x", [128, NM], BF16, 2)
        ys = Slots(k, "lr_y", [128, NM], BF16, 2)
        xcs = Slots(k, "lr_xc", [128, NM], F32, 1)
        xbs = Slots(k, "lr_xb", [128, NM], BF16, 1)
        rs_ = Slots(k, "lr_r", [128, NM], F32, 1)
        is_ = Slots(k, "lr_i", [128, NM], F32, 1)
        as_ = Slots(k, "lr_a", [128, NM], F32, 1)
        us_ = Slots(k, "lr_u", [128, NM], F32, 1)
        hs_ = [Slots(k, "lr_h%d" % d, [128, NM], F32, 1) for d in range(2)]
        gys = Slots(k, "lr_gy", [128, NM], F32, 1)
        obs = Slots(k, "lr_ob", [128, NM], BF16, 2)
        for (t0, N, lat, si) in c.seqs():
            tis = list(range(t0 // 512, (t0 + N + 511) // 512))
            rPF = [k.bPF[i] for i in tis]
            for j in range(LB):
                x_, bx = xs.next()
                y_, by = ys.next()
                em.dma("sp", x_[:, 0:N], k.PF[c.f_lx + j, :, t0:t0 + N], reads=rPF, writes=[bx])
                em.dma("sp", y_[:, 0:N], k.PF[c.f_ly + j, :, t0:t0 + N], reads=rPF, writes=[by])
                xc, bxc = xcs.next()
                cw = lambda tap: vcol(k, "convw%d" % l, tap * LB + j, 1)
                em.op("dve", lambda: nc.vector.tensor_scalar(xc[:, 0:N], x_[:, 0:N], cw(2), vcol(k, "convb%d" % l, j, 1),
                                                             ALU.mult, ALU.add), reads=[bx, k.bvec], writes=[bxc])
                em.op("dve", lambda: nc.vector.scalar_tensor_tensor(xc[:, 2:N], x_[:, 0:N - 2], cw(0), xc[:, 2:N],
                                                                    ALU.mult, ALU.add), reads=[bx, bxc, k.bvec], writes=[bxc])
                em.op("dve", lambda: nc.vector.scalar_tensor_tensor(xc[:, 1:N], x_[:, 0:N - 1], cw(1), xc[:, 1:N],
                                                                    ALU.mult, ALU.add), reads=[bx, bxc, k.bvec], writes=[bxc])
                em.op("dve", lambda: nc.vector.scalar_tensor_tensor(xc[:, 0:N - 1], x_[:, 1:N], cw(3), xc[:, 0:N - 1],
                                                                    ALU.mult, ALU.add), reads=[bx, bxc, k.bvec], writes=[bxc])
                xb, bxb = xbs.next()
                em.op("act", lambda: nc.scalar.copy(xb[:, 0:N], xc[:, 0:N]), reads=[bxc], writes=[bxb])
                hh = []
                for d in range(2):
                    r_, br_ = rs_.next()
                    i_, bi_ = is_.next()
                    G = min(N, 512)
                    for n0 in range(0, N, G):
                        for (wt, dst, bdst, bname) in ((wr, r_, br_, "br%d"), (wi, i_, bi_, "bi%d")):
                            pb, bp = pbank(k)
                            em.op("pe", lambda pb=pb, wt=wt, n0=n0: nc.tensor.matmul(
                                pb[:, 0:G], lhsT=wt[:, d * LB + j, :], rhs=xb[:, n0:n0 + G], start=True, stop=True),
                                reads=[bw, bxb], writes=[bp])
                            em.op("act", lambda pb=pb, dst=dst, n0=n0, bname=bname: nc.scalar.activation(
                                dst[:, n0:n0 + G], pb[:, 0:G], AF.Sigmoid, bias=vcol(k, bname % l, d * LB + j, 1), scale=1.0),
                                reads=[bp, k.bvec], writes=[bdst])
                    a_, ba_ = as_.next()
                    u_, bu_ = us_.next()
                    em.op("act", lambda: nc.scalar.activation(a_[:, 0:N], r_[:, 0:N], AF.Exp,
                                                              scale=cl[:, d * LB + j:d * LB + j + 1]),
                          reads=[br_, bw], writes=[ba_])
                    em.op("pool", lambda: nc.gpsimd.tensor_tensor(u_[:, 0:N], a_[:, 0:N], a_[:, 0:N], ALU.mult),
                          reads=[ba_], writes=[bu_])
                    em.op("act", lambda: nc.scalar.activation(u_[:, 0:N], u_[:, 0:N], AF.Sqrt, bias=k.onet[:], scale=-1.0),
                          reads=[bu_, k.bconst], writes=[bu_])
                    em.op("pool", lambda: nc.gpsimd.tensor_tensor(i_[:, 0:N], i_[:, 0:N], xc[:, 0:N], ALU.mult),
                          reads=[bi_, bxc], writes=[bi_])
                    em.op("dve", lambda: nc.vector.tensor_tensor(u_[:, 0:N], u_[:, 0:N], i_[:, 0:N], ALU.mult),
                          reads=[bu_, bi_], writes=[bu_])
                    h_, bh_ = hs_[d].next()
                    init = vcol(k, "lru0%d" % l, d * LB + j, 1) if lat else 0.0
                    if d == 0:
                        em.op("dve", lambda: nc.vector.tensor_tensor_scan(h_[:, 0:N], a_[:, 0:N], u_[:, 0:N], init,
                                                                          ALU.mult, ALU.add),
                              reads=[ba_, bu_, k.bvec], writes=[bh_])
                    else:
                        em.op("dve", lambda: nc.vector.tensor_tensor_scan(rev(h_[:, 0:N]), rev(a_[:, 0:N]), rev(u_[:, 0:N]),
                                                                          init, ALU.mult, ALU.add),
                              reads=[ba_, bu_, k.bvec], writes=[bh_])
                    if not lat:
                        col = (si * 2 + d) * LB + j
                        src = h_[:, N - 1:N] if d == 0 else h_[:, 0:1]
                        em.op("pool", lambda src=src, col=col: nc.gpsimd.tensor_copy(fs[:, col:col + 1], src),
                              reads=[bh_], writes=[bfs])
                    hh.append((h_, bh_))
                gy, bgy = gys.next()
                em.op("act", lambda: nc.scalar.activation(gy[:, 0:N], y_[:, 0:N], AF.Gelu_apprx_tanh), reads=[by], writes=[bgy])
                em.op("pool", lambda: nc.gpsimd.tensor_tensor(hh[0][0][:, 0:N], hh[0][0][:, 0:N], hh[1][0][:, 0:N], ALU.add),
                      reads=[hh[0][1], hh[1][1]], writes=[hh[0][1]])
                ob, bob = obs.next()
                em.op("dve", lambda: nc.vector.tensor_tensor(ob[:, 0:N], hh[0][0][:, 0:N], gy[:, 0:N], ALU.mult),
                      reads=[hh[0][1], bgy], writes=[bob])
                em.dma("pool", k.MX[c.AH + c.GH + j, :, t0:t0 + N], ob[:, 0:N], reads=[bob],
                       writes=[k.bMX[i] for i in tis])
        ncol = c.NSEQ * 2 * LB
        pb, bp = pbank(k)
        em.op("pe", lambda: nc.tensor.transpose(pb[0:ncol, 0:128], fs[:, 0:ncol], k.identf[:]),
              reads=[bfs, k.bconst], writes=[bp])
        fso = k.sb("lr_fso", [128, 128], F32)
        bfo = Buf("fso")
        em.op("dve", lambda: nc.vector.tensor_copy(fso[0:ncol, :], pb[0:ncol, 0:128]), reads=[bp], writes=[bfo])
        for si in range(c.NSEQ):
            em.dma("pool", k.nlru[si, l].rearrange("d (j p) -> (d j) p", p=128), fso[si * 2 * LB:(si + 1) * 2 * LB, :],
                   reads=[bfo])


def resid_gemm(k, l, src, bsrc, KCin, W, gate_off, wcols, nslots_a, pfx, k0=0):
    nc, em, c = k.nc, k.em, k.c
    src = src[k0:k0 + KCin]
    W = W[k0 * 128:(k0 + KCin) * 128, :]
    with phase(k):
        acts = Slots(k, pfx + "a", [128, KCin, 512], BF16, nslots_a)
        ws = Slots(k, pfx + "w", [128, KCin, wcols], BF16, 3)
        xs = Slots(k, pfx + "x", [128, wcols // 128, 512], F32, 3)
        for (t0s, r) in c.tile_pairs(nslots_a):
            ats = [load_act_tile(k, acts, src, bsrc, KCin, t0) for t0 in t0s]
            for g0 in range(0, c.D, wcols):
                wt, bw = load_w(k, ws, W, KCin, g0, wcols)
                nj = wcols // 128
                m0 = g0 // 128
                for t0, (at, ba) in zip(t0s, ats):
                    ti = t0 // 512
                    xt, bx = xs.next()
                    em.dma("sp", xt[:], k.XT[m0:m0 + nj, :, t0:t0 + 512].rearrange("c p t -> p c t"),
                           reads=[k.bXT[ti]], writes=[bx])
                    for j in range(nj):
                        pb, bp = mm_fm(k, wt, bw, at, ba, KCin, j)
                        gate = k.mod[:, l, gate_off + m0 + j, r:r + 1]
                        em.op("dve", lambda pb=pb, j=j, gate=gate, xt=xt: nc.vector.scalar_tensor_tensor(
                            xt[:, j, :], pb[:], gate, xt[:, j, :], ALU.mult, ALU.add), reads=[bp, bx, k.bmod], writes=[bx])
                    em.dma("sp", k.XT[m0:m0 + nj, :, t0:t0 + 512].rearrange("c p t -> p c t"), xt[:],
                           reads=[bx], writes=[k.bXT[ti]])


def phase_outproj(k, l):
    resid_gemm(k, l, k.MX, k.bMX, k.c.KCM, k.w_out[l], 2 * k.c.KC, min(512, k.c.D), 2, "op_")


def phase_down(k, l):
    FC = k.c.FC
    h = (FC + 1) // 2
    if FC <= 48:
        resid_gemm(k, l, k.AC, k.bAC, FC, k.w_down[l], 5 * k.c.KC, min(256, k.c.D), 2, "dn_")
    else:
        resid_gemm(k, l, k.AC, k.bAC, h, k.w_down[l], 5 * k.c.KC, 256, 2, "dna_", k0=0)
        resid_gemm(k, l, k.AC, k.bAC, FC - h, k.w_down[l], 5 * k.c.KC, 256, 2, "dnb_", k0=h)


def phase_gu(k, l):
    nc, em, c = k.nc, k.em, k.c
    KC = c.KC
    W = k.w_gu[l]
    with phase(k):
        acts = Slots(k, "gu_a", [128, KC, 512], BF16, 2)
        GWD = 256
        wg = Slots(k, "gu_g", [128, KC, GWD], BF16, 3)
        wu = Slots(k, "gu_u", [128, KC, GWD], BF16, 3)
        sgs = Slots(k, "gu_s", [128, 512], F32, 3)
        stg = Slots(k, "gu_o", [128, GWD // 128, 512], BF16, 3)
        bg = None
        if l + 1 < c.L:
            k.nbanks = 7
            bg = ada_bg(k, l + 1, Slots(k, "gu_ada", [128, KC, 128], BF16, 3), k.pb[7], k.bpb[7])
            nsteps = len(c.tile_pairs()) * ((c.DFF + GWD - 1) // GWD)
            per = -(-(6 * KC) // nsteps)
        for (t0s, r) in c.tile_pairs():
            ats = [load_act_tile(k, acts, k.HN, k.bHN, KC, t0) for t0 in t0s]
            for f0 in range(0, c.DFF, GWD):
                if bg is not None:
                    for _ in range(per):
                        next(bg, None)
                n = min(GWD, c.DFF - f0)
                nj = n // 128
                wgt, bwg = load_w(k, wg, W, KC, f0, n)
                wut, bwu = load_w(k, wu, W, KC, c.DFF + f0, n)
                for t0, (at, ba) in zip(t0s, ats):
                    ti = t0 // 512
                    so, bso = stg.next()
                    for j in range(nj):
                        pg, bpg = mm_fm(k, wgt, bwg, at, ba, KC, j)
                        pu, bpu = mm_fm(k, wut, bwu, at, ba, KC, j)
                        sg_, bsg = sgs.next()
                        em.op("act", lambda pg=pg, sg_=sg_: nc.scalar.activation(sg_[:], pg[:], AF.Silu), reads=[bpg], writes=[bsg])
                        em.op("dve", lambda pu=pu, sg_=sg_, j=j, so=so: nc.vector.tensor_tensor(so[:, j, :], sg_[:], pu[:], ALU.mult),
                              reads=[bsg, bpu], writes=[bso])
                    em.dma("sp", k.AC[f0 // 128:f0 // 128 + nj, :, t0:t0 + 512].rearrange("c p t -> p c t"), so[:, 0:nj, :],
                           reads=[bso], writes=[k.bAC[ti]])
        if bg is not None:
            for _ in bg:
                pass
            k.nbanks = 8


def phase_final(k):
    nc, em, c = k.nc, k.em, k.c
    KC = c.KC
    with phase(k):
        xs = Slots(k, "fn_x", [128, KC, 512], F32, 2)
        sqs = Slots(k, "fn_q", [128, KC, 512], BF16, 1)
        rss = Slots(k, "fn_r", [128, 512], F32, 2)
        ys = Slots(k, "fn_y", [128, c.D], F32, 2)
        gb = bc_last(vcol(k, "fng", 0, KC), 512)
        ei = 0
        for (t0, r) in c.tiles():
            ti = t0 // 512
            xt, bx = xs.next()
            em.dma("sp", xt[:], k.XT[:, :, t0:t0 + 512].rearrange("c p t -> p c t"), reads=[k.bXT[ti]], writes=[bx])
            sq, bsq = sqs.next()
            rs, brs = rss.next()
            norm_block(k, xt, bx, 512, sq, bsq, rs, brs)
            em.op("dve", lambda: nc.vector.tensor_tensor(xt[:], xt[:], bc_mid(rs[:], KC), ALU.mult),
                  reads=[bx, brs], writes=[bx])
            em.op("dve", lambda: nc.vector.tensor_tensor(xt[:], xt[:], gb, ALU.mult), reads=[bx, k.bvec], writes=[bx])
            for blk in range(4):
                yt, by = ys.next()
                for c0 in range(0, KC, 4):
                    n = min(4, KC - c0)
                    pb, bp = pbank(k)
                    em.group("pe", [
                        (lambda j=j, pb=pb: nc.tensor.transpose(pb[:, j * 128:(j + 1) * 128],
                                                                xt[:, c0 + j, blk * 128:(blk + 1) * 128], k.identf[:]))
                        for j in range(n)], reads=[bx, k.bconst], writes=[bp])
                    evac(k, ei, yt[:, c0 * 128:(c0 + n) * 128], pb[:, 0:n * 128], [bp], [by])
                    ei += 1
                tok = t0 + blk * 128
                em.dma("pool", k.y[tok:tok + 128, :], yt[:], reads=[by])


def host_consts(c):
    ident = np.eye(128, dtype=np.float32)
    pm = np.zeros((128, 128), np.float32)
    for p in range(128):
        j = p % 64
        partner = p + 32 if j < 32 else p - 32
        pm[partner, p] = 1.0
    GRID_W = 64
    rows = c.DSEQ // GRID_W
    row = np.repeat(np.arange(rows, dtype=np.float32), GRID_W)
    col = np.tile(np.arange(GRID_W, dtype=np.float32), rows)
    inv = (np.float32(10000.0) ** (-np.arange(32, dtype=np.float32) / np.float32(32))).astype(np.float32)
    rope = np.zeros((128, 2, c.DSEQ), np.float32)
    for p in range(128):
        half, j = p // 64, p % 64
        pos = row if half == 0 else col
        i = j % 32
        ang = (pos * inv[i]).astype(np.float32)
        rope[p, 0] = np.cos(ang)
        rope[p, 1] = -np.sin(ang) if j < 32 else np.sin(ang)
    kj = np.arange(128)[:, None]
    qi = np.arange(128)[None, :]
    mL = np.where(kj >= qi, 0.0, -30000.0).astype(np.float32)
    mR = np.where(kj <= qi, 0.0, -30000.0).astype(np.float32)
    mlr = np.stack([np.tile(mL, (1, 4)), np.tile(mR, (1, 4))], axis=1)
    s_ = np.arange(64)[:, None]
    t_ = np.arange(64)[None, :]
    gF = (s_ <= t_).astype(np.float32)
    gB = (s_ >= t_).astype(np.float32)
    gmask = np.stack([np.tile(gF, (1, 8)), np.tile(gB, (1, 8))], axis=1)
    reset = np.ones((128, 2048), np.float32)
    reset[:, 0::64] = 0.0
    return dict(c_ident=ident, c_pm=pm, c_rope=rope, c_mlr=np.ascontiguousarray(mlr),
                c_gmask=np.ascontiguousarray(gmask), c_reset=reset)


def host_vecs(c, inp, core):
    v = np.zeros((128, c.NV), np.float32)

    def put(name, arr2d):
        o = c.vcol[name]
        v[:, o:o + arr2d.shape[1]] = arr2d
    cT = np.stack([feat_major(inp["c_ctx"]), feat_major(inp["c"][core])], axis=2)
    put("cT", cT.reshape(128, -1))
    put("fng", feat_major(inp["final_norm_g"]))
    for l in range(c.L):
        put("bada%d" % l, feat_major(inp["b_ada"][l]))
        put("n1g%d" % l, feat_major(inp["norm1_g"][l]))
        put("n2g%d" % l, feat_major(inp["norm2_g"][l]))
        put("sink%d" % l, np.broadcast_to(np.asarray(inp["attn_sink"][l], np.float32)[None, :], (128, c.AH)))
        put("bdec%d" % l, np.concatenate([feat_major(inp["gla_b_decay"][l, d]) for d in range(2)], axis=1))
        put("gnorm%d" % l, feat_major(inp["gla_norm_g"][l]))
        put("convw%d" % l, np.concatenate([feat_major(inp["lru_conv_w"][l, t]) for t in range(4)], axis=1))
        put("convb%d" % l, feat_major(inp["lru_conv_b"][l]))
        put("br%d" % l, np.concatenate([feat_major(inp["lru_b_rgate"][l, d]) for d in range(2)], axis=1))
        put("bi%d" % l, np.concatenate([feat_major(inp["lru_b_igate"][l, d]) for d in range(2)], axis=1))
        put("lam%d" % l, np.concatenate([feat_major(inp["lru_lambda"][l, d]) for d in range(2)], axis=1))
        put("lru0%d" % l, np.concatenate([feat_major(inp["state_lru"][core, l, d]) for d in range(2)], axis=1))
    return v


_NC_CACHE = {}


def run(c, inp, n_cores):
    key = (c.D, c.AH, c.KV, c.GH, c.LB, c.DFF, c.SEQ, c.NSEQ, c.DSEQ, c.PAST, c.L)
    if key not in _NC_CACHE:
        _NC_CACHE[key] = build(c)
    nc = _NC_CACHE[key]
    f32 = lambda a: np.ascontiguousarray(np.asarray(a, np.float32))
    consts = host_consts(c)
    shared = dict(
        w_ada=f32(inp["w_ada"]), w_in=f32(inp["w_in"]), w_dec=f32(inp["gla_w_decay"]),
        w_r=f32(inp["lru_w_rgate"]), w_i=f32(inp["lru_w_igate"]), w_out=f32(inp["w_out"]),
        w_gu=f32(inp["w_gu"]), w_down=f32(inp["w_down"]), **consts)
    xp = f32(inp["x_prompt"])
    xs = f32(inp["x_sample"])
    in_maps = []
    for i in range(n_cores):
        m = dict(shared)
        m["xin"] = np.concatenate([xp[i * c.NSEQ:(i + 1) * c.NSEQ].reshape(c.NCT, c.D), xs[i]], axis=0)
        m["ck"] = f32(inp["cache_attn_k"][i]).reshape(c.L, c.PAST, c.KW)
        m["cv"] = f32(inp["cache_attn_v"][i]).reshape(c.L, c.PAST, c.KW)
        m["sg"] = f32(inp["state_gla"][i])
        m["vecs"] = host_vecs(c, inp, i)
        in_maps.append(m)
    res = run_bass_kernel_spmd(nc, in_maps, core_ids=list(range(n_cores))).results
    B = n_cores * c.NSEQ
    y_prompt = np.concatenate([r["y"][:c.NCT].reshape(c.NSEQ, c.SEQ, c.D) for r in res], axis=0)
    y_sample = np.stack([r["y"][c.NCT:] for r in res], axis=0)
    nk = np.concatenate([r["nk"].reshape(c.NSEQ, c.L, c.SEQ, c.KV, 128) for r in res], axis=0)
    nv = np.concatenate([r["nv"].reshape(c.NSEQ, c.L, c.SEQ, c.KV, 128) for r in res], axis=0)
    ngla = np.concatenate([r["ngla"] for r in res], axis=0)
    nlru = np.concatenate([r["nlru"] for r in res], axis=0)
    return (y_prompt.astype(np.float32), y_sample.astype(np.float32), nk.astype(np.float32),
            nv.astype(np.float32), ngla.astype(np.float32), nlru.astype(np.float32))


def kernel(**inputs):
    return run(Cfg(), inputs, 8)
```
